# Optimizing a Trainium2 kernel written in Bass

```python
import jax, jax.numpy as jnp
from jax import lax
import numpy as np

D_MODEL = 2048
BATCH = 1
SEQ = 8192
DEPTH = 1

SB_HEADS = 16
SB_HEAD_DIM = 128
SB_BLOCK = 128
GDN_HEADS = 16
GDN_K_DIM = 128
GDN_V_DIM = 128
GDN_CONV = 4
GDN_CHUNK = 64
D_FF = -(-8 * D_MODEL // (3 * 256)) * 256
EPS = 1e-6

D_SB = SB_HEADS * SB_HEAD_DIM
D_GDN_K = GDN_HEADS * GDN_K_DIM
D_GDN_V = GDN_HEADS * GDN_V_DIM
D_GDN_QKV = 2 * D_GDN_K + D_GDN_V
IN_SPLITS = (D_SB, D_SB, D_SB, D_GDN_QKV, D_GDN_V, GDN_HEADS, GDN_HEADS, D_MODEL, D_MODEL)
IN_OFFSETS = tuple(int(o) for o in np.cumsum(IN_SPLITS)[:-1])
D_IN = int(sum(IN_SPLITS))

kernel_name = 'hybrid_stickbreak_gdn_adaln_block'


def _rms(x, w):
    xf = x.astype(jnp.float32)
    return xf * lax.rsqrt(jnp.mean(xf * xf, axis=-1, keepdims=True) + EPS) * w.astype(jnp.float32)


def _l2norm(x):
    return x * lax.rsqrt(jnp.sum(x * x, axis=-1, keepdims=True) + EPS)


def _heads(t, n, d):
    b, s, _ = t.shape
    return t.reshape(b, s, n, d).transpose(0, 2, 1, 3)


def _merge_heads(t):
    b, n, s, d = t.shape
    return t.transpose(0, 2, 1, 3).reshape(b, s, n * d)


def _causal_depthwise_conv(x, w):
    return lax.conv_general_dilated(
        x, w[:, None, :].astype(x.dtype), window_strides=(1,), padding=[(GDN_CONV - 1, 0)],
        dimension_numbers=('NWC', 'WIO', 'NWC'), feature_group_count=x.shape[-1])


def _stick_breaking(q, k, v):
    s_len, d = q.shape[2], q.shape[3]
    scale = d ** -0.5
    outs = []
    for i in range(s_len // SB_BLOCK):
        q0 = i * SB_BLOCK
        kend = q0 + SB_BLOCK
        z = jnp.einsum('bhqd,bhkd->bhqk', q[:, :, q0:kend], k[:, :, :kend]) * scale
        causal = jnp.arange(kend)[None, :] < (q0 + jnp.arange(SB_BLOCK))[:, None]
        log_1m = jnp.where(causal, jax.nn.log_sigmoid(-z), 0.0)
        after = lax.cumsum(log_1m, axis=3, reverse=True) - log_1m
        w = jnp.where(causal, jnp.exp(jax.nn.log_sigmoid(z) + after), 0.0)
        outs.append(jnp.einsum('bhqk,bhkd->bhqd', w, v[:, :, :kend]))
    return jnp.concatenate(outs, axis=2)


def _gated_delta_rule(q, k, v, g, beta):
    b, h, s_len, dk = q.shape
    dv = v.shape[-1]
    n, c = s_len // GDN_CHUNK, GDN_CHUNK
    q = q.reshape(b, h, n, c, dk)
    k = k.reshape(b, h, n, c, dk)
    v = v.reshape(b, h, n, c, dv)
    g = g.reshape(b, h, n, c)
    beta = beta.reshape(b, h, n, c)
    gc = jnp.cumsum(g, axis=-1)
    tril = jnp.tril(jnp.ones((c, c), dtype=bool))
    stril = jnp.tril(jnp.ones((c, c), dtype=bool), k=-1)
    diff = gc[..., :, None] - gc[..., None, :]
    decay = jnp.where(tril, jnp.exp(jnp.where(tril, diff, 0.0)), 0.0)
    kk = jnp.einsum('bhnrd,bhnid->bhnri', k, k)
    m = jnp.eye(c, dtype=q.dtype) + jnp.where(stril, beta[..., :, None] * kk * decay, 0.0)
    solve = lambda rhs: lax.linalg.triangular_solve(m, rhs, left_side=True, lower=True, unit_diagonal=True)
    w_v = solve(beta[..., None] * v)
    w_k = solve((beta * jnp.exp(gc))[..., None] * k)
    attn = jnp.einsum('bhnrd,bhnid->bhnri', q, k) * decay
    q_g = q * jnp.exp(gc)[..., None]
    k_dec = k * jnp.exp(gc[..., -1:] - gc)[..., None]
    g_last = jnp.exp(gc[..., -1])

    def step(state, xs):
        w_v_c, w_k_c, q_g_c, attn_c, k_dec_c, g_last_c = xs
        u = w_v_c - jnp.einsum('bhcd,bhde->bhce', w_k_c, state)
        o = jnp.einsum('bhcd,bhde->bhce', q_g_c, state) + jnp.einsum('bhcj,bhje->bhce', attn_c, u)
        state = g_last_c[..., None, None] * state + jnp.einsum('bhcd,bhce->bhde', k_dec_c, u)
        return state, o

    mv = lambda t: jnp.moveaxis(t, 2, 0)
    state0 = jnp.zeros((b, h, dk, dv), dtype=q.dtype)
    _, o = lax.scan(step, state0, (mv(w_v), mv(w_k), mv(q_g), mv(attn), mv(k_dec), mv(g_last)))
    return jnp.moveaxis(o, 0, 2).reshape(b, h, s_len, dv)


def setup_inputs(seed: int = 0) -> dict:
    key = jax.random.key(seed)
    ks = jax.random.split(key, 20)
    f32 = jnp.float32
    nrm = lambda k, shape, s: jax.random.normal(k, shape, f32) * s
    gain = lambda k, shape: 1.0 + 0.02 * jax.random.normal(k, shape, f32)
    dt = jnp.exp(jax.random.uniform(ks[10], (DEPTH, GDN_HEADS), f32, np.log(1e-3), np.log(1e-1)))
    return {
        'x': nrm(ks[0], (BATCH, SEQ, D_MODEL), 1.0),
        'c': nrm(ks[1], (BATCH, D_MODEL), 1.0),
        'w_mod': nrm(ks[2], (DEPTH, D_MODEL, 6 * D_MODEL), 0.5 * D_MODEL ** -0.5),
        'b_mod': nrm(ks[3], (DEPTH, 6 * D_MODEL), 0.02),
        'norm1_w': gain(ks[4], (DEPTH, D_MODEL)),
        'w_in': nrm(ks[5], (DEPTH, D_MODEL, D_IN), D_MODEL ** -0.5),
        'q_norm_w': gain(ks[6], (DEPTH, SB_HEAD_DIM)),
        'k_norm_w': gain(ks[7], (DEPTH, SB_HEAD_DIM)),
        'conv_w': nrm(ks[8], (DEPTH, GDN_CONV, D_GDN_QKV), GDN_CONV ** -0.5),
        'a_log': jnp.log(jax.random.uniform(ks[9], (DEPTH, GDN_HEADS), f32, 1.0, 16.0)),
        'dt_bias': dt + jnp.log(-jnp.expm1(-dt)),
        'o_norm_w': gain(ks[11], (DEPTH, GDN_V_DIM)),
        'p_a': nrm(ks[12], (DEPTH, D_SB, D_MODEL), D_SB ** -0.5),
        'p_b': nrm(ks[13], (DEPTH, D_GDN_V, D_MODEL), D_GDN_V ** -0.5),
        'w_out': nrm(ks[14], (DEPTH, D_MODEL, D_MODEL), D_MODEL ** -0.5),
        'norm2_w': gain(ks[15], (DEPTH, D_MODEL)),
        'w_gate': nrm(ks[16], (DEPTH, D_MODEL, D_FF), D_MODEL ** -0.5),
        'w_up': nrm(ks[17], (DEPTH, D_MODEL, D_FF), D_MODEL ** -0.5),
        'w_down': nrm(ks[18], (DEPTH, D_FF, D_MODEL), D_FF ** -0.5),
    }


def reference(x, c, w_mod, b_mod, norm1_w, w_in, q_norm_w, k_norm_w, conv_w, a_log, dt_bias,
              o_norm_w, p_a, p_b, w_out, norm2_w, w_gate, w_up, w_down):
    f32 = jnp.float32
    bsz, s_len, _ = x.shape
    h = x.astype(f32)
    c_act = jax.nn.silu(c.astype(f32))
    for l in range(DEPTH):
        mod = c_act @ w_mod[l].astype(f32) + b_mod[l].astype(f32)
        shift1, scale1, gate1, shift2, scale2, gate2 = [t[:, None, :] for t in jnp.split(mod, 6, axis=-1)]

        u = _rms(h, norm1_w[l]) * (1.0 + scale1) + shift1
        proj = u @ w_in[l].astype(f32)
        qa, ka, va, qkv_b, z_b, b_b, a_b, gate_a, gate_b = jnp.split(proj, IN_OFFSETS, axis=-1)

        qa = _rms(_heads(qa, SB_HEADS, SB_HEAD_DIM), q_norm_w[l])
        ka = _rms(_heads(ka, SB_HEADS, SB_HEAD_DIM), k_norm_w[l])
        va = _heads(va, SB_HEADS, SB_HEAD_DIM)
        o_a = _merge_heads(_stick_breaking(qa, ka, va))

        qkv_b = jax.nn.silu(_causal_depthwise_conv(qkv_b, conv_w[l]))
        qb, kb, vb = jnp.split(qkv_b, (D_GDN_K, 2 * D_GDN_K), axis=-1)
        qb = _l2norm(_heads(qb, GDN_HEADS, GDN_K_DIM)) * (GDN_K_DIM ** -0.5)
        kb = _l2norm(_heads(kb, GDN_HEADS, GDN_K_DIM))
        vb = _heads(vb, GDN_HEADS, GDN_V_DIM)
        beta = jax.nn.sigmoid(b_b).transpose(0, 2, 1)
        g = (-jnp.exp(a_log[l].astype(f32)) * jax.nn.softplus(a_b + dt_bias[l].astype(f32))).transpose(0, 2, 1)
        o_b = _gated_delta_rule(qb, kb, vb, g, beta)
        o_b = _rms(o_b, o_norm_w[l]) * jax.nn.silu(_heads(z_b, GDN_HEADS, GDN_V_DIM))
        o_b = _merge_heads(o_b)

        merged = (jax.nn.sigmoid(gate_a) * (o_a @ p_a[l].astype(f32))
                  + jax.nn.sigmoid(gate_b) * (o_b @ p_b[l].astype(f32)))
        h = h + gate1 * (merged @ w_out[l].astype(f32))

        u = _rms(h, norm2_w[l]) * (1.0 + scale2) + shift2
        ff = jax.nn.silu(u @ w_gate[l].astype(f32)) * (u @ w_up[l].astype(f32))
        h = h + gate2 * (ff @ w_down[l].astype(f32))
    return h.astype(x.dtype)
```

```python
import numpy as np
from contextlib import ExitStack
import concourse.bass as bass
import concourse.mybir as mybir
from concourse.bass_utils import run_bass_kernel_spmd

F32 = mybir.dt.float32
BF16 = mybir.dt.bfloat16
AF = mybir.ActivationFunctionType
ALU = mybir.AluOpType

D = 2048
NK = 16
DFF = 5632
NF = 44
EPS = 1e-6
NCORE = 8
WH = 1796


_RECORD = False
_NEEDED = {}
_RANK = {}


class Trk:
    __slots__ = ("w", "r", "dsem", "dval", "dq")

    def __init__(s):
        s.w = {}
        s.r = {}
        s.dsem = None
        s.dval = 0
        s.dq = None


class V:
    __slots__ = ("ap", "k")

    def __init__(s, ap, k):
        s.ap = ap
        s.k = k


class TT:
    def __init__(s, t, k=None):
        s.t = t
        s.k = k if k is not None else Trk()

    def __getitem__(s, idx):
        return V(s.t[idx], s.k)


class Eng:
    def __init__(s, K, name, eng, same):
        s.K = K
        s.name = name
        s.eng = eng
        s.same = same
        s.own = set()
        s.sid = K.newsem(name)
        K.tl[s.sid] = name
        s.own.add(s.sid)
        s.cnt = 0
        s.last = None
        s.seen = {}
        s.pend_r = []
        s.pend_w = []

    def _wait(s, sid, val):
        if s.seen.get(sid, 0) >= val:
            return
        s.seen[sid] = val
        nm = s.K.tl.get(sid)
        if nm is not None:
            if _RECORD:
                _NEEDED.setdefault(nm, set()).add(val)
            else:
                val = _RANK[nm][val]
        s.eng.wait_ge(s.K.h[sid], val)

    def issue(s, fn, r, w, dma_k=None, nowaw=False, defer=False):
        need = {}
        for t in r:
            for sid, val in t.w.items():
                if sid in s.own and not s.same:
                    continue
                if need.get(sid, 0) < val:
                    need[sid] = val
        for t in w:
            if not nowaw:
                for sid, val in t.w.items():
                    if sid in s.own and not s.same:
                        continue
                    if need.get(sid, 0) < val:
                        need[sid] = val
            for sid, val in t.r.items():
                if sid in s.own and not s.same:
                    continue
                if need.get(sid, 0) < val:
                    need[sid] = val
        for sid, val in need.items():
            s._wait(sid, val)
        ins = fn()
        if dma_k is not None:
            k = dma_k
            s.K.all_dma.add(k)
            if k.dsem is None:
                fl = s.K.free_dsems.setdefault(s.name, [])
                k.dq = s.name
                if fl:
                    k.dsem, k.dval = fl.pop()
                else:
                    k.dsem = s.K.newsem("d")
            assert k.dq == s.name, (k.dq, s.name)
            k.dval += 16
            ins.then_inc(s.K.h[k.dsem], 16)
            tok = (k.dsem, k.dval)
        else:
            if defer:
                s.pend_r += r
                s.pend_w += w
                return
            s.cnt += 1
            if _RECORD or s.cnt in _RANK.get(s.name, {}):
                ins.then_inc(s.K.h[s.sid], 1)
            tok = (s.sid, s.cnt)
            s.last = tok
            r = r + s.pend_r
            w = w + s.pend_w
            s.pend_r = []
            s.pend_w = []
        for t in r:
            if t.r.get(tok[0], 0) < tok[1]:
                t.r[tok[0]] = tok[1]
        for t in w:
            if nowaw:
                t.w[tok[0]] = tok[1]
            else:
                t.w = {tok[0]: tok[1]}
            t.r = {}


class Kern:
    def __init__(s, nc, es):
        s.nc = nc
        s.es = es
        s.h = []
        s.uid = 0
        s.free_dsems = {}
        s.all_dma = set()
        s.tl = {}
        s.PE = Eng(s, "pe", nc.tensor, False)
        s.ACT = Eng(s, "act", nc.scalar, True)
        s.DVE = Eng(s, "dve", nc.vector, True)
        s.POOL = Eng(s, "pool", nc.gpsimd, True)
        s.SP = Eng(s, "sp", nc.sync, True)
        s.engs = [s.PE, s.ACT, s.DVE, s.POOL, s.SP]

    def newsem(s, name):
        s.uid += 1
        sem = s.es.enter_context(s.nc.semaphore(f"{name}{s.uid}"))
        s.h.append(sem)
        return len(s.h) - 1

    def sb(s, shape, dt, es=None, name="t"):
        s.uid += 1
        t = (es or s.es).enter_context(s.nc.sbuf_tensor(f"{name}{s.uid}", list(shape), dt))
        o = TT(t)
        if es is not None:
            def _rel(k=o.k, K=s):
                if k.dsem is not None:
                    K.free_dsems.setdefault(k.dq, []).append((k.dsem, k.dval))
                    k.dsem = None
            es.callback(_rel)
        return o

    def barrier(s):
        for e in s.engs:
            for o in s.engs:
                if o is not e and o.last is not None:
                    e._wait(o.last[0], o.last[1])

    def do(s, E, meth, outs, ins, nowaw=False, defer=False, **kw):
        r = [v.k for v in ins.values() if isinstance(v, V)]
        w = [v.k for v in outs.values()]
        args = {}
        for k_, v in list(outs.items()) + list(ins.items()):
            args[k_] = v.ap if isinstance(v, V) else v
        args.update(kw)
        E.issue(lambda: getattr(E.eng, meth)(**args), r, w, nowaw=nowaw, defer=defer)

    def mm(s, out, lhsT, rhs, start=True, stop=True, defer=False):
        s.do(s.PE, "matmul", dict(out=out), dict(lhsT=lhsT, rhs=rhs), start=start, stop=stop, defer=defer)

    def tr(s, out, in_, ident, defer=False):
        s.do(s.PE, "transpose", dict(out=out), dict(in_=in_, identity=ident), defer=defer)

    def act(s, out, in_, func, scale=1.0, bias=0.0, accum=None):
        outs = dict(out=out)
        if accum is not None:
            outs["accum_out"] = accum
        s.do(s.ACT, "activation", outs, dict(in_=in_, bias=bias, scale=scale), func=func)

    def ts(s, E, out, in0, s1, s2, op0, op1=None):
        kw = dict(op0=op0)
        if op1 is not None:
            kw["op1"] = op1
        s.do(E, "tensor_scalar", dict(out=out), dict(in0=in0, scalar1=s1, scalar2=s2), **kw)

    def stt(s, out, in0, scalar, in1, op0, op1):
        s.do(s.DVE, "scalar_tensor_tensor", dict(out=out), dict(in0=in0, scalar=scalar, in1=in1), op0=op0, op1=op1)

    def tt(s, E, out, in0, in1, op):
        s.do(E, "tensor_tensor", dict(out=out), dict(in0=in0, in1=in1), op=op)

    def copy(s, E, out, in_):
        if E is s.ACT:
            s.act(out, in_, AF.Identity)
        else:
            s.do(E, "tensor_copy", dict(out=out), dict(in_=in_))

    def memset(s, E, out, val):
        s.do(E, "memset", dict(ap=out), {}, constant=val)

    def dma(s, E, out, in_, nowaw=False):
        r = [in_.k]
        w = [out.k]
        E.issue(lambda: E.eng.dma_start(out=out.ap, in_=in_.ap), r, w, dma_k=out.k, nowaw=nowaw)


class _Stop(Exception):
    pass


import os
STOP = os.environ.get("KSTOP") or None
SIMDBG = False
ATT_SKEW = 0
GDN_MODE = 0


def build(SEQ=8192):
    global _RECORD, _NEEDED, _RANK
    _RECORD, _NEEDED, _RANK = True, {}, {}
    try:
        _build(SEQ)
    except _Stop:
        pass
    _RANK = {nm: {v: i + 1 for i, v in enumerate(sorted(vs))} for nm, vs in _NEEDED.items()}
    _RECORD = False
    try:
        return _build(SEQ)
    except _Stop as e:
        return e.args[0]


def _build(SEQ=8192):
    NT = SEQ // 128
    NG = SEQ // 512
    TOWN = SEQ // NCORE
    NHALF = TOWN // 512
    nc = bass.Bass("TRN2", target_bir_lowering=False)

    def din(name, shape):
        return TT(nc.dram_tensor(name, list(shape), F32, kind="ExternalInput").ap())

    x_all = din("x_all", [SEQ, D])
    x_own = din("x_own", [TOWN, D])
    c_col = din("c_col", [128, NK])
    w_mod = din("w_mod", [D, 1536])
    bmod_col = din("bmod_col", [128, 12])
    msend = TT(nc.dram_tensor("msend", [128, 64], F32).ap())
    mrecv = TT(nc.dram_tensor("mrecv", [8 * 128, 64], F32).ap())
    n1w = din("n1w_col", [128, NK])
    n2w = din("n2w_col", [128, NK])
    w_h = din("w_h", [D, WH])
    qnw = din("qnw_col", [128, 1])
    knw = din("knw_col", [128, 1])
    convw = din("convw", [128, 24])
    alog_bc = din("alog_bc", [128, 2])
    dtb_bc = din("dtb_bc", [128, 2])
    onw_bc = din("onw_bc", [128, 128])
    if STOP is None:
        w_g = din("w_g", [D, 2 * D])
        p_a = din("p_a", [D, D])
        p_b = din("p_b", [D, D])
        w_out = din("w_out", [D, D])
        w_gate = din("w_gate", [D, DFF])
        w_up = din("w_up", [D, DFF])
        w_down = din("w_down", [DFF, D])
    NCST = 6 * 128 + 4 * 512 + 2
    cst = din("cst", [128, NCST])
    out_d = TT(nc.dram_tensor("out", [TOWN, D], F32, kind="ExternalOutput").ap())
    if SIMDBG:
        modc_dbg = din("modc_dbg", [128, 96])
        dbg = TT(nc.dram_tensor("dbg", [128, 8192], F32, kind="ExternalOutput").ap())
    send = TT(nc.dram_tensor("sendb", [8 * 512, TOWN], BF16).ap())
    recv = TT(nc.dram_tensor("recvb", [8 * 8 * 512, TOWN], BF16).ap())

    with ExitStack() as es:
        K = Kern(nc, es)
        PE, ACT, DVE, POOL, SP = K.PE, K.ACT, K.DVE, K.POOL, K.SP

        def dump(v, c0, n):
            if SIMDBG:
                POOL.issue(lambda: nc.gpsimd.dma_start(out=dbg.t[:, c0:c0 + n], in_=v.ap), [v.k], [dbg.k],
                           dma_k=dbg.k, nowaw=True)

        def chk(name):
            if STOP == name:
                K.barrier()
                for k_ in K.all_dma:
                    if k_.dsem is not None:
                        (POOL if k_.dq == "pool" else SP)._wait(k_.dsem, k_.dval)
                for q_, fl_ in K.free_dsems.items():
                    for sid_, val_ in fl_:
                        (POOL if q_ == "pool" else SP)._wait(sid_, val_)
                print("STOP at", name, "sems", len(K.h), [(e.name, e.cnt) for e in K.engs])
                raise _Stop(nc)
        mm, tr, act, ts, stt, tt, copy, memset, dma = K.mm, K.tr, K.act, K.ts, K.stt, K.tt, K.copy, K.memset, K.dma

        PSt = [es.enter_context(nc.psum_tensor(f"ps{i}", [128, 512], F32)) for i in range(7)]
        PS = [TT(t) for t in PSt]
        PSBt = es.enter_context(nc.psum_tensor("psb", [128, 1024], BF16))
        _pk = Trk()
        PSB = [TT(PSBt, _pk), TT(PSBt, _pk)]

        def pslot(i, c0, c1, p0=0, p1=128):
            class _S:
                pass
            o = _S()
            o.k = PS[i].k

            def gi(idx, _i=i, _c0=c0):
                ps, cs = idx
                a = _c0 + (cs.start or 0)
                b = _c0 + (cs.stop if cs.stop is not None else (c1 - c0))
                return V(PSt[_i][ps, a:b], o.k)
            o.g = gi
            return o

        ident_f = K.sb([128, 128], F32)
        ident_b = K.sb([128, 128], BF16)
        ones_f = K.sb([128, 128], F32)
        nones_f = K.sb([128, 128], F32)
        ones_b = K.sb([128, 128], BF16)
        nones_b = K.sb([128, 128], BF16)
        tri2 = K.sb([128, 128], F32)
        bones = K.sb([128, 128], F32)
        mLs = K.sb([128, 128], F32)
        mUi = K.sb([128, 128], F32)
        negU = K.sb([128, 128], BF16)
        mdiag = K.sb([128, 4, 512], BF16)
        sel = K.sb([128, 2], F32)
        with ExitStack() as e0:
            cs_t = K.sb([128, NCST], F32, e0)
            dma(SP, cs_t[:, :], cst[:, :])
            copy(DVE, ident_f[:, :], cs_t[:, 0:128])
            copy(DVE, ident_b[:, :], cs_t[:, 0:128])
            copy(DVE, tri2[:, :], cs_t[:, 128:256])
            copy(DVE, bones[:, :], cs_t[:, 256:384])
            copy(DVE, mLs[:, :], cs_t[:, 384:512])
            copy(DVE, mUi[:, :], cs_t[:, 512:640])
            copy(DVE, negU[:, :], cs_t[:, 640:768])
            for r_ in range(4):
                copy(DVE, mdiag[:, r_, :], cs_t[:, 768 + 512 * r_:768 + 512 * (r_ + 1)])
            copy(DVE, sel[:, :], cs_t[:, 768 + 2048:768 + 2050])
            memset(DVE, ones_f[:, :], 1.0)
            memset(DVE, nones_f[:, :], -1.0)
            memset(DVE, ones_b[:, :], 1.0)
            memset(DVE, nones_b[:, :], -1.0)
            K.barrier()

        modc = K.sb([128, 96], F32)
        a1 = K.sb([128, NK], F32)
        a2 = K.sb([128, NK], F32)
        small = K.sb([128, 64], F32)
        with ExitStack() as e0:
            cc = K.sb([128, NK], F32, e0)
            cb = K.sb([128, NK], BF16, e0)
            bm = K.sb([128, 12], F32, e0)
            modl = K.sb([128, 64], F32, e0)
            memset(DVE, modl[:, :], 0.0)
            nw1 = K.sb([128, NK], F32, e0)
            nw2 = K.sb([128, NK], F32, e0)
            slabs = [K.sb([128, NK, 512], BF16, e0) for _ in range(3)]
            dma(SP, cc[:, :], c_col[:, :])
            if not SIMDBG:
                dma(SP, bm[:, :], bmod_col[:, :])
            dma(SP, nw1[:, :], n1w[:, :])
            dma(SP, nw2[:, :], n2w[:, :])
            act(cb[:, :], cc[:, :], AF.Silu)
            wm = w_mod.t.rearrange("(k p) n -> p k n", p=128)
            pm = PS[0]
            for sl in range(0 if SIMDBG else 3):
                S_ = slabs[sl % 3]
                dma(POOL, S_[:, :, :], V(wm[:, :, sl * 512:(sl + 1) * 512], w_mod.k))
                for jb in range(4):
                    j = sl * 4 + jb
                    for k in range(NK):
                        mm(pm[:, j:j + 1], S_[:, k, jb * 128:(jb + 1) * 128], cb[:, k:k + 1],
                           start=(k == 0), stop=(k == NK - 1), defer=(k != NK - 1))
            if SIMDBG:
                dma(SP, modc[:, :], modc_dbg[:, :])
            else:
              tt(DVE, modl[:, 0:12], pm[:, 0:12], bm[:, :], ALU.add)
              dma(SP, msend[:, :], modl[:, :])
              POOL.issue(lambda: nc.gpsimd.collective_compute(
                "AllGather", ALU.bypass, replica_groups=[list(range(NCORE))], ins=[msend.t], outs=[mrecv.t]),
                [msend.k], [mrecv.k])
              POOL._wait(POOL.sid, POOL.cnt)
              dma(SP, V(modc.t[:, :].rearrange("p (r j) -> p r j", r=8), modc.k),
                  V(mrecv.t.rearrange("(r p) j -> p r j", p=128)[:, :, 0:12], mrecv.k))
            stt(a1[:, :], modc[:, 16:32], 1.0, nw1[:, :], ALU.add, ALU.mult)
            stt(a2[:, :], modc[:, 64:80], 1.0, nw2[:, :], ALU.add, ALU.mult)
            K.barrier()
        chk("p0")
        b1 = lambda k: modc[:, k:k + 1]
        b2 = lambda k: modc[:, 48 + k:49 + k]

        def make_uT(es_, load_tile, n_tiles, acol, bcol, uT):
            xts = [K.sb([128, D], F32, es_) for _ in range(2)]
            xns = [K.sb([128, D], BF16, es_) for _ in range(4)]
            st = K.sb([128, 8], F32, es_)
            for t in range(n_tiles):
                xt = load_tile(t, xts[t % 2])
                xn = xns[t % 4]
                act(xn[:, :], xt[:, :], AF.Square, accum=st[:, 0:1])
                ts(DVE, st[:, 1:2], st[:, 0:1], 1.0 / D, EPS, ALU.mult, ALU.add)
                act(st[:, 2:3], st[:, 1:2], AF.Ln)
                act(st[:, 3:4], st[:, 2:3], AF.Exp, scale=-0.5)
                ts(DVE, xn[:, :], xt[:, :], st[:, 3:4], None, ALU.mult)
            for k in range(NK):
                pb = PSB[k % 2]
                c0 = (k % 2) * 512
                for t in range(n_tiles):
                    tr(V(PSBt[:, c0 + t * 128:c0 + (t + 1) * 128], pb.k), xns[t % 4][:, k * 128:(k + 1) * 128],
                       ident_b[:, :], defer=(t != n_tiles - 1))
                src = V(PSBt[:, c0:c0 + n_tiles * 128], pb.k)
                if k % 2 == 0:
                    ts(DVE, uT[:, k, :], src, acol[:, k:k + 1], bcol(k), ALU.mult, ALU.add)
                else:
                    act(uT[:, k, :], src, AF.Identity, scale=acol[:, k:k + 1], bias=bcol(k))

        with ExitStack() as e1:
            Wh = K.sb([128, NK, WH], BF16, e1)
            whv = w_h.t.rearrange("(k p) n -> p k n", p=128)
            for k in range(NK):
                dma(POOL, Wh[:, k, :], V(whv[:, k, :], w_h.k), nowaw=True)
            KT = [K.sb([128, SEQ], BF16, e1) for _ in range(2)]
            VA = [K.sb([128, NT, 128], BF16, e1) for _ in range(2)]
            uT = K.sb([128, NK, 512], BF16, e1)
            qT = [K.sb([128, 512], BF16, e1) for _ in range(2)]
            qbT = [K.sb([128, 512], BF16, e1) for _ in range(2)]
            kbT = [K.sb([128, 512], BF16, e1) for _ in range(2)]
            vbT = [K.sb([128, 512], BF16, e1) for _ in range(2)]
            hist = [K.sb([128, 4], F32, e1) for _ in range(6)]
            siluz = K.sb([128, 4, 256], BF16, e1)
            betag = K.sb([128, 4, 4], F32, e1)
            stage = [K.sb([128, 512], BF16, e1) for _ in range(4)]
            Sf = [K.sb([128, 128], F32, e1) for _ in range(2)]
            Sb = [K.sb([128, 128], BF16, e1) for _ in range(2)]
            cw = K.sb([128, 24], F32, e1)
            qcol = K.sb([128, 4], F32, e1)
            negea = K.sb([128, 2], F32, e1)
            dtb = K.sb([128, 2], F32, e1)
            onw = K.sb([128, 128], F32, e1)
            dma(SP, cw[:, :], convw[:, :])
            dma(SP, qcol[:, 2:3], qnw[:, :])
            dma(SP, qcol[:, 1:2], knw[:, :])
            dma(SP, negea[:, :], alog_bc[:, :])
            dma(SP, dtb[:, :], dtb_bc[:, :])
            dma(SP, onw[:, :], onw_bc[:, :])
            ts(DVE, qcol[:, 0:1], qcol[:, 2:3], float(128 ** -0.5), None, ALU.mult)
            act(negea[:, :], negea[:, :], AF.Exp)
            ts(DVE, negea[:, :], negea[:, :], -1.0, None, ALU.mult)
            for h_ in hist:
                memset(DVE, h_[:, :], 0.0)
            for h in range(2):
                memset(DVE, Sf[h][:, :], 0.0)
                memset(DVE, Sb[h][:, :], 0.0)
            K.barrier()
            sendv = send.t.rearrange("(t a h p) c -> t a h p c", t=8, a=2, h=2)

            for g in range(NG):
                with ExitStack() as ea:
                    def load_x(t, dst, _g=g):
                        ti = _g * 4 + t
                        dma(SP, dst[:, :], x_all[ti * 128:(ti + 1) * 128, :])
                        return dst
                    make_uT(ea, load_x, 4, a1, b1, uT)
                    if g == 0:
                        dump(uT[:, 0, :], 0, 512)
                        dump(uT[:, 5, :], 512, 512)
                    K.barrier()
                    chk("1a")
                with ExitStack() as eb:
                    F = [K.sb([128, 512], F32, eb) for _ in range(4)]
                    raw = [K.sb([128, 516], F32, eb) for _ in range(2)]
                    sqb = [K.sb([128, 512], BF16, eb) for _ in range(2)]
                    sm = K.sb([128, 16], F32, eb)
                    for blk in range(10):
                        P = PS[blk % 4]
                        for k in range(NK):
                            mm(P[:, :], Wh[:, k, blk * 128:(blk + 1) * 128], uT[:, k, :],
                               start=(k == 0), stop=(k == NK - 1), defer=(k != NK - 1))
                        if blk < 4:
                            h = blk % 2
                            isq = blk < 2
                            sq = sqb[blk % 2]
                            act(sq[:, :], P[:, :], AF.Square)
                            Pss = PS[4 + blk % 2]
                            mm(Pss[:, :], ones_b[:, :], sq[:, :])
                            lnv = F[0]
                            rs = F[1]
                            ts(DVE, lnv[:, :], Pss[:, :], 1.0 / 128, EPS, ALU.mult, ALU.add)
                            act(lnv[:, :], lnv[:, :], AF.Ln)
                            act(rs[:, :], lnv[:, :], AF.Exp, scale=-0.5)
                            dst = qT[h][:, :] if isq else KT[h][:, g * 512:(g + 1) * 512]
                            stt(dst, P[:, :], qcol[:, 0:1] if isq else qcol[:, 1:2], rs[:, :], ALU.mult, ALU.mult)
                        else:
                            s_ = blk - 4
                            typ = s_ // 2
                            h = s_ % 2
                            rw = raw[s_ % 2]
                            copy(ACT, rw[:, 3:515], P[:, :])
                            copy(DVE, rw[:, 0:3], hist[s_][:, 0:3])
                            acc = F[2]
                            ci = (h * 3 + typ) * 4
                            ts(DVE, acc[:, :], rw[:, 0:512], cw[:, ci:ci + 1], None, ALU.mult)
                            for j in range(1, 4):
                                stt(acc[:, :], rw[:, j:j + 512], cw[:, ci + j:ci + j + 1], acc[:, :], ALU.mult, ALU.add)
                            copy(DVE, hist[s_][:, 0:3], rw[:, 512:515])
                            if typ == 2:
                                act(vbT[h][:, :], acc[:, :], AF.Silu)
                            else:
                                slt = F[3]
                                act(slt[:, :], acc[:, :], AF.Silu)
                                sq = sqb[blk % 2]
                                act(sq[:, :], slt[:, :], AF.Square)
                                Pss = PS[4 + blk % 2]
                                mm(Pss[:, :], ones_b[:, :], sq[:, :])
                                lnv = F[0]
                                rs = F[1]
                                ts(DVE, lnv[:, :], Pss[:, :], EPS, None, ALU.add)
                                act(lnv[:, :], lnv[:, :], AF.Ln)
                                act(rs[:, :], lnv[:, :], AF.Exp, scale=-0.5)
                                dst = qbT[h] if typ == 0 else kbT[h]
                                stt(dst[:, :], slt[:, :], float(128 ** -0.5) if typ == 0 else 1.0, rs[:, :],
                                    ALU.mult, ALU.mult)
                    for t in range(4):
                        ti = g * 4 + t
                        P1 = PS[t % 2]
                        P2 = PS[2 + t % 2]
                        for k in range(NK):
                            mm(P1[:, :], uT[:, k, t * 128:(t + 1) * 128], Wh[:, k, 1280:1792],
                               start=(k == 0), stop=(k == NK - 1), defer=True)
                        for k in range(NK):
                            mm(P2[:, 0:4], uT[:, k, t * 128:(t + 1) * 128], Wh[:, k, 1792:1796],
                               start=(k == 0), stop=(k == NK - 1), defer=(k != NK - 1))
                        copy(ACT, VA[0][:, ti, :], P1[:, 0:128])
                        copy(DVE, VA[1][:, ti, :], P1[:, 128:256])
                        act(siluz[:, t, :], P1[:, 256:512], AF.Silu)
                        act(betag[:, t, 0:2], P2[:, 0:2], AF.Sigmoid)
                        tt(DVE, sm[:, 0:2], P2[:, 2:4], dtb[:, :], ALU.add)
                        act(sm[:, 2:4], sm[:, 0:2], AF.Exp)
                        act(sm[:, 4:6], sm[:, 2:4], AF.Ln, bias=1.0)
                        tt(DVE, betag[:, t, 2:4], sm[:, 4:6], negea[:, :], ALU.mult)
                    if g == 0:
                        dump(qT[1][:, :], 1024, 512)
                        dump(KT[0][:, 0:512], 1536, 512)
                        dump(qbT[0][:, :], 2048, 512)
                        dump(kbT[1][:, :], 2560, 512)
                        dump(vbT[0][:, :], 3072, 512)
                        dump(VA[1][:, 1, :], 3584, 128)
                        dump(siluz[:, 2, :], 3712, 256)
                        dump(V(betag.t[:, :, :].rearrange("p a b -> p (a b)"), betag.k), 3968, 16)
                    K.barrier()
                    chk("1b")
                with ExitStack() as ec:
                    Ebuf = [K.sb([128, 512], F32, ec) for _ in range(2)]
                    SPb = [K.sb([128, 512], BF16, ec) for _ in range(2)]
                    tmpb = [K.sb([128, 512], F32, ec) for _ in range(2)]
                    Wb = [K.sb([128, 512], BF16, ec) for _ in range(2)]
                    Rbc = K.sb([128, 512], F32, ec)
                    for h in range(2):
                        memset(DVE, Rbc[:, :], 0.0)
                        PO = PS[6]
                        js = list(range(4 * g + 3, -1, -1))
                        nj = len(js)

                        def stA(idx, h=h, js=js):
                            j = js[idx]
                            b = idx % 2
                            r_ = j - 4 * g
                            kt = KT[h][:, j * 128:(j + 1) * 128]
                            mm(PS[b][:, :], kt, qT[h][:, :])
                            act(Ebuf[b][:, :], PS[b][:, :], AF.Exp)
                            act(SPb[b][:, :], Ebuf[b][:, :], AF.Ln, bias=1.0)
                            if r_ >= 0:
                                tt(DVE, SPb[b][:, :], SPb[b][:, :], mdiag[:, r_, :], ALU.mult)

                        def stB(idx, h=h, js=js):
                            j = js[idx]
                            b = idx % 2
                            r_ = j - 4 * g
                            PA, PB = PS[2 + b], PS[4 + b]
                            kt = KT[h][:, j * 128:(j + 1) * 128]
                            S_ = SPb[b]
                            mm(PA[:, :], kt, qT[h][:, :], start=True, stop=False, defer=True)
                            mm(PA[:, :], negU[:, :], S_[:, :], start=False, stop=True)
                            mm(PB[:, :], nones_b[:, :], S_[:, :])
                            T_ = tmpb[b]
                            tt(DVE, T_[:, :], PA[:, :], Rbc[:, :], ALU.add)
                            tt(DVE, Rbc[:, :], PB[:, :], Rbc[:, :], ALU.add)
                            W_ = Wb[b]
                            act(W_[:, :], T_[:, :], AF.Exp)
                            if r_ >= 0:
                                tt(DVE, W_[:, :], W_[:, :], mdiag[:, r_, :], ALU.mult)

                        def stC(idx, h=h, js=js, nj=nj):
                            j = js[idx]
                            b = idx % 2
                            mm(PO[:, :], VA[h][:, j, :], Wb[b][:, :], start=(idx == 0), stop=(idx == nj - 1))

                        if ATT_SKEW:
                            for s_ in range(nj + 2):
                                if s_ < nj:
                                    stA(s_)
                                if 0 <= s_ - 1 < nj:
                                    stB(s_ - 1)
                                if 0 <= s_ - 2 < nj:
                                    stC(s_ - 2)
                        else:
                            for s_ in range(nj):
                                stA(s_)
                                stB(s_)
                                stC(s_)
                        copy(ACT, stage[h][:, :], PO[:, :])
                        dma(SP, V(sendv[(g * 512) // TOWN, 0, h, :, (g * 512) % TOWN:(g * 512) % TOWN + 512], send.k),
                            stage[h][:, :], nowaw=True)
                    if g == 0:
                        dump(stage[0][:, :], 4096, 512)
                        dump(stage[1][:, :], 4608, 512)
                    K.barrier()
                    chk("1c")
                with ExitStack() as ed:
                    def f128(n, dt=F32):
                        return [K.sb([128, 128], dt, ed) for _ in range(n)]
                    G1, dl, du, Lm, Um, Ym, wv, tq, oraw, on_ = (f128(2) for _ in range(10))
                    PA_, PB_ = f128(4), f128(4)
                    attnT, Yb, wkT, kbt, kdec, vbt, ub, obb = (f128(2, BF16) for _ in range(8))
                    sc = K.sb([128, 32], F32, ed)
                    Pg = pslot(0, 0, 16)
                    SL = []
                    for h in range(2):
                        bank = 1 + h * 3
                        SL.append(dict(
                            Pdiff=pslot(bank, 0, 128), Pkk=pslot(bank, 128, 256), Pqk=pslot(bank, 256, 384),
                            PU=pslot(bank, 384, 512),
                            NP=[pslot(bank + 1, i * 128, (i + 1) * 128) for i in range(4)],
                            CP=[pslot(bank + 2, i * 128, (i + 1) * 128) for i in range(4)]))
                    for t in range(4):
                        ti = g * 4 + t
                        cols = slice(t * 128, (t + 1) * 128)
                        gsel = sc[:, 0:4]
                        for c_ in range(2):
                            ts(DVE, sc[:, 2 * c_:2 * c_ + 2], betag[:, t, 2:4], sel[:, c_:c_ + 1], None, ALU.mult)
                        mm(Pg.g((slice(None), slice(0, 2))), tri2[:, :], betag[:, t, 2:4])
                        mm(Pg.g((slice(None), slice(2, 4))), bones[:, :], betag[:, t, 2:4])
                        mm(Pg.g((slice(None), slice(4, 8))), ones_f[:, :], gsel)
                        gcs = sc[:, 4:12]
                        copy(ACT, gcs, Pg.g((slice(None), slice(0, 8))))
                        act(sc[:, 12:14], sc[:, 4:6], AF.Exp)
                        tt(DVE, sc[:, 14:16], sc[:, 6:8], sc[:, 4:6], ALU.subtract)
                        act(sc[:, 14:16], sc[:, 14:16], AF.Exp)
                        act(sc[:, 16:20], sc[:, 8:12], AF.Exp)
                        tt(DVE, sc[:, 20:22], betag[:, t, 0:2], sc[:, 12:14], ALU.mult)
                        if g == 0 and t == 0:
                            chk("d1")
                        def gdn_head(h, t=t, cols=cols, ti=ti):
                            beta_c = betag[:, t, h:h + 1]
                            g_c = betag[:, t, 2 + h:3 + h]
                            egc_c = sc[:, 12 + h:13 + h]
                            ekd_c = sc[:, 14 + h:15 + h]
                            bege_c = sc[:, 20 + h:21 + h]
                            Pdiff, Pkk, Pqk, PU = SL[h]["Pdiff"], SL[h]["Pkk"], SL[h]["Pqk"], SL[h]["PU"]
                            NP, CP = SL[h]["NP"], SL[h]["CP"]
                            A_ = (slice(None), slice(0, 128))
                            kT_t = kbT[h][:, cols]
                            qT_t = qbT[h][:, cols]
                            ts(DVE, G1[h][:, :], tri2[:, :], g_c, None, ALU.mult)
                            mm(Pdiff.g(A_), G1[h][:, :], ones_f[:, :], start=True, stop=False, defer=True)
                            mm(Pdiff.g(A_), nones_f[:, :], G1[h][:, :], start=False, stop=True)
                            mm(Pkk.g(A_), kT_t, kT_t)
                            mm(Pqk.g(A_), kT_t, qT_t)
                            yield
                            ts(DVE, dl[h][:, :], Pdiff.g(A_), 0.0, None, ALU.min)
                            ts(DVE, du[h][:, :], Pdiff.g(A_), -1.0, 0.0, ALU.mult, ALU.min)
                            act(dl[h][:, :], dl[h][:, :], AF.Exp)
                            act(du[h][:, :], du[h][:, :], AF.Exp)
                            yield
                            tt(DVE, dl[h][:, :], dl[h][:, :], mLs[:, :], ALU.mult)
                            tt(DVE, du[h][:, :], du[h][:, :], mUi[:, :], ALU.mult)
                            stt(Lm[h][:, :], Pkk.g(A_), beta_c, dl[h][:, :], ALU.mult, ALU.mult)
                            tt(DVE, attnT[h][:, :], Pqk.g(A_), du[h][:, :], ALU.mult)
                            yield
                            if g == 0 and t == 0 and h == 0:
                                chk("d2")
                            tr(PU.g(A_), Lm[h][:, :], ident_f[:, :])
                            copy(ACT, Um[h][:, :], PU.g(A_))
                            tt(DVE, Ym[h][:, :], ident_f[:, :], Um[h][:, :], ALU.subtract)
                            yield
                            if g == 0 and t == 0 and h == 0:
                                chk("d2b")
                            Pc, Ptc = Um[h], Lm[h]
                            for it in range(5):
                                Pn, Ptn = PA_[2 * h + it % 2], PB_[2 * h + it % 2]
                                s0, s1_, s2_ = NP[0], NP[1], NP[2]
                                if it < 4:
                                    mm(s0.g(A_), Ptc[:, :], Pc[:, :])
                                mm(s1_.g(A_), Pc[:, :], Ptc[:, :])
                                copy(ACT, Ptn[:, :], s1_.g(A_))
                                if it < 4:
                                    copy(ACT, Pn[:, :], s0.g(A_))
                                yield
                                mm(s2_.g(A_), Ptn[:, :], Ym[h][:, :])
                                tt(DVE, Ym[h][:, :], Ym[h][:, :], s2_.g(A_), ALU.add)
                                yield
                                Pc, Ptc = Pn, Ptn
                                if g == 0 and t == 0 and h == 0 and it == 0:
                                    chk("d2c")
                            copy(ACT, Yb[h][:, :], Ym[h][:, :])
                            yield
                            if g == 0 and t == 0 and h == 0:
                                chk("d3")
                            pb0, pb1 = PSB[0], PSB[1]
                            tr(V(PSBt[:, 0:128], pb0.k), kT_t, ident_b[:, :])
                            ts(DVE, kbt[h][:, :], V(PSBt[:, 0:128], pb0.k), bege_c, None, ALU.mult)
                            ts(DVE, kdec[h][:, :], V(PSBt[:, 0:128], pb0.k), ekd_c, None, ALU.mult)
                            yield
                            tr(V(PSBt[:, 512:640], pb1.k), vbT[h][:, cols], ident_b[:, :])
                            ts(DVE, vbt[h][:, :], V(PSBt[:, 512:640], pb1.k), beta_c, None, ALU.mult)
                            yield
                            mm(CP[0].g(A_), Yb[h][:, :], vbt[h][:, :])
                            copy(ACT, wv[h][:, :], CP[0].g(A_))
                            mm(CP[1].g(A_), kbt[h][:, :], Yb[h][:, :])
                            copy(ACT, wkT[h][:, :], CP[1].g(A_))
                            yield
                            if g == 0 and t == 0 and h == 0:
                                chk("d4")
                            for c_ in range(2):
                                pr = slice(c_ * 64, c_ * 64 + 64)
                                Ap = (pr, slice(0, 128))
                                Pu_, Pqs, Pau, PSs = CP[2], CP[3], NP[3], CP[0]
                                mm(Pu_.g(Ap), wkT[h][:, pr], Sb[h][:, :])
                                mm(Pqs.g(Ap), qbT[h][:, t * 128 + c_ * 64:t * 128 + c_ * 64 + 64], Sb[h][:, :])
                                tt(DVE, ub[h][pr, :], wv[h][pr, :], Pu_.g(Ap), ALU.subtract)
                                yield
                                mm(PSs.g(A_), kdec[h][pr, :], ub[h][pr, :])
                                mm(Pau.g(Ap), attnT[h][pr, pr], ub[h][pr, :])
                                act(tq[h][pr, :], Pqs.g(Ap), AF.Identity, scale=V(egc_c.ap[pr, :], egc_c.k))
                                tt(DVE, oraw[h][pr, :], tq[h][pr, :], Pau.g(Ap), ALU.add)
                                yield
                                egl_c = sc[:, 16 + 2 * c_ + h:17 + 2 * c_ + h]
                                stt(Sb[h][:, :], Sf[h][:, :], egl_c, PSs.g(A_), ALU.mult, ALU.add)
                                stt(Sf[h][:, :], Sf[h][:, :], egl_c, PSs.g(A_), ALU.mult, ALU.add)
                                yield
                                if g == 0 and t == 0 and h == 0 and c_ == 0:
                                    chk("d5a")
                                if g == 0 and t == 0 and h == 0 and c_ == 1:
                                    chk("d5")
                            act(on_[h][:, :], oraw[h][:, :], AF.Square, accum=sc[:, 22 + h:23 + h])
                            ts(DVE, sc[:, 24 + h:25 + h], sc[:, 22 + h:23 + h], 1.0 / 128, EPS, ALU.mult, ALU.add)
                            act(sc[:, 26 + h:27 + h], sc[:, 24 + h:25 + h], AF.Ln)
                            act(sc[:, 28 + h:29 + h], sc[:, 26 + h:27 + h], AF.Exp, scale=-0.5)
                            stt(on_[h][:, :], oraw[h][:, :], sc[:, 28 + h:29 + h], onw[:, :], ALU.mult, ALU.mult)
                            yield
                            tt(DVE, obb[h][:, :], on_[h][:, :], siluz[:, t, h * 128:(h + 1) * 128], ALU.mult)
                            tr(V(PSBt[:, 128:256], pb0.k), obb[h][:, :], ident_b[:, :])
                            copy(DVE, stage[2 + h][:, cols], V(PSBt[:, 128:256], pb0.k))
                            if g == 0 and t == 0 and h == 0:
                                chk("d6")
                        gens = [gdn_head(0), gdn_head(1)]
                        if GDN_MODE == 0:
                            for gen_ in gens:
                                for _ in gen_:
                                    pass
                            gens = []
                        while gens:
                            for gen_ in list(gens):
                                try:
                                    next(gen_)
                                except StopIteration:
                                    gens.remove(gen_)
                    for h in range(2):
                        dma(SP, V(sendv[(g * 512) // TOWN, 1, h, :, (g * 512) % TOWN:(g * 512) % TOWN + 512], send.k),
                            stage[2 + h][:, :], nowaw=True)
                    if g == 0:
                        dump(stage[2][:, :], 5120, 512)
                        dump(stage[3][:, :], 5632, 512)
                    K.barrier()
                    chk("1d")
            K.barrier()

        POOL.issue(lambda: nc.gpsimd.collective_compute(
            "AllGather", ALU.bypass, replica_groups=[list(range(NCORE))], ins=[send.t], outs=[recv.t]),
            [send.k], [recv.k])
        POOL._wait(POOL.sid, POOL.cnt)
        chk("ag")
        rank = nc.gpsimd.partition_id()
        rv = recv.t.rearrange("(s t a h p) c -> t a p s h c", s=8, t=8, a=2, h=2)

        with ExitStack() as e2:
            g1bc = K.sb([128, D], F32, e2)
            g2bc = K.sb([128, D], F32, e2)
            with ExitStack() as eg:
                dg = [K.sb([128, 128], F32, eg) for _ in range(2)]
                for which, dst in ((32, g1bc), (80, g2bc)):
                    for j in range(NK):
                        d_ = dg[j % 2]
                        ts(DVE, d_[:, :], ident_f[:, :], modc[:, which + j:which + j + 1], None, ALU.mult)
                        P = PS[j % 2]
                        mm(P[:, 0:128], ones_f[:, :], d_[:, :])
                        copy(ACT, dst[:, j * 128:(j + 1) * 128], P[:, 0:128])
                K.barrier()
            for half in range(NHALF):
                hc = slice(half * 512, half * 512 + 512)
                with ExitStack() as eh:
                    h1 = [K.sb([128, D], F32, eh) for _ in range(4)]
                    mT = K.sb([128, NK, 512], BF16, eh)
                    with ExitStack() as eb:
                        oaT = K.sb([128, NK, 512], BF16, eb)
                        obT = K.sb([128, NK, 512], BF16, eb)
                        uTo = K.sb([128, NK, 512], BF16, eb)
                        for ab, dst in ((0, oaT), (1, obT)):
                            src = rv[bass.ds(rank, 1), ab].rearrange("o p s h c -> (o p) s h c")
                            for hh in range(2):
                                POOL.issue(lambda _d=dst, _s=src, _h=hh: nc.gpsimd.dma_start(
                                    out=_d.t[:, :, :].rearrange("p (s h) c -> p s h c", s=8)[:, :, _h, :],
                                    in_=_s[:, :, _h, hc]),
                                    [recv.k], [dst.k], dma_k=dst.k, nowaw=True)
                        with ExitStack() as ea:
                            def load_xo(t, dst, _half=half):
                                r0 = _half * 512 + t * 128
                                dma(SP, dst[:, :], x_own[r0:r0 + 128, :])
                                return dst
                            make_uT(ea, load_xo, 4, a1, b1, uTo)
                            K.barrier()
                        for t in range(4):
                            r0 = half * 512 + t * 128
                            dma(SP, h1[t][:, :], x_own[r0:r0 + 128, :])
                        with ExitStack() as ew:
                            slab = [[K.sb([128, NK, 256], BF16, ew) for _ in range(4)] for _ in range(2)]
                            Fm = [K.sb([128, 512], F32, ew) for _ in range(4)]
                            srcs = [(p_a, 0), (p_b, 0), (w_g, 0), (w_g, D)]
                            for c2 in range(8):
                                sl_ = slab[c2 % 2]
                                for wi, (wt, off) in enumerate(srcs):
                                    wvv = wt.t.rearrange("(k p) n -> p k n", p=128)
                                    dma(POOL, sl_[wi][:, :, :], V(wvv[:, :, off + c2 * 256:off + (c2 + 1) * 256], wt.k))
                                for cc_ in range(2):
                                    jc = c2 * 2 + cc_
                                    wc = slice(cc_ * 128, cc_ * 128 + 128)
                                    Ps = [PS[(jc % 2) * 3 + i] if i < 3 else PS[6] for i in range(4)]
                                    acts = [oaT, obT, uTo, uTo]
                                    for wi in range(4):
                                        for k in range(NK):
                                            mm(Ps[wi][:, :], sl_[wi][:, k, wc], acts[wi][:, k, :],
                                               start=(k == 0), stop=(k == NK - 1), defer=(k != NK - 1))
                                    act(Fm[0][:, :], Ps[2][:, :], AF.Sigmoid)
                                    act(Fm[1][:, :], Ps[3][:, :], AF.Sigmoid)
                                    tt(DVE, Fm[2][:, :], Fm[0][:, :], Ps[0][:, :], ALU.mult)
                                    tt(DVE, Fm[3][:, :], Fm[1][:, :], Ps[1][:, :], ALU.mult)
                                    tt(DVE, mT[:, jc, :], Fm[2][:, :], Fm[3][:, :], ALU.add)
                            K.barrier()
                    with ExitStack() as ew:
                        slab = [K.sb([128, NK, 512], BF16, ew) for _ in range(2)]
                        Fm = [K.sb([128, 512], F32, ew) for _ in range(2)]
                        wvv = w_out.t.rearrange("(k p) n -> p k n", p=128)
                        for cg in range(4):
                            sl_ = slab[cg % 2]
                            cs_ = slice(cg * 512, cg * 512 + 512)
                            dma(POOL, sl_[:, :, :], V(wvv[:, :, cs_], w_out.k))
                            for t in range(4):
                                P = PS[(cg * 4 + t) % 4]
                                for k in range(NK):
                                    mm(P[:, :], mT[:, k, t * 128:(t + 1) * 128], sl_[:, k, :],
                                       start=(k == 0), stop=(k == NK - 1), defer=(k != NK - 1))
                                f_ = Fm[t % 2]
                                tt(DVE, f_[:, :], P[:, :], g1bc[:, cs_], ALU.mult)
                                tt(DVE, h1[t][:, cs_], f_[:, :], h1[t][:, cs_], ALU.add)
                        K.barrier()
                    with ExitStack() as ef:
                        u2T = K.sb([128, NK, 512], BF16, ef)
                        ffT = K.sb([128, NF, 512], BF16, ef)
                        with ExitStack() as ea:
                            def load_h(t, dst):
                                return h1[t]
                            make_uT(ea, load_h, 4, a2, b2, u2T)
                            K.barrier()
                        with ExitStack() as ew:
                            slab = [[K.sb([128, NK, 256], BF16, ew) for _ in range(2)] for _ in range(2)]
                            Fm = [K.sb([128, 512], F32, ew) for _ in range(2)]
                            wgv = w_gate.t.rearrange("(k p) n -> p k n", p=128)
                            wuv = w_up.t.rearrange("(k p) n -> p k n", p=128)
                            for f2 in range(NF // 2):
                                sl_ = slab[f2 % 2]
                                cs_ = slice(f2 * 256, f2 * 256 + 256)
                                dma(POOL, sl_[0][:, :, :], V(wgv[:, :, cs_], w_gate.k))
                                dma(POOL, sl_[1][:, :, :], V(wuv[:, :, cs_], w_up.k))
                                for cc_ in range(2):
                                    f = f2 * 2 + cc_
                                    wc = slice(cc_ * 128, cc_ * 128 + 128)
                                    Pg_, Pu2 = PS[(f % 2) * 2], PS[(f % 2) * 2 + 1]
                                    for wi, P in ((0, Pg_), (1, Pu2)):
                                        for k in range(NK):
                                            mm(P[:, :], sl_[wi][:, k, wc], u2T[:, k, :],
                                               start=(k == 0), stop=(k == NK - 1), defer=(k != NK - 1))
                                    f_ = Fm[f % 2]
                                    act(f_[:, :], Pg_[:, :], AF.Silu)
                                    tt(DVE, ffT[:, f, :], f_[:, :], Pu2[:, :], ALU.mult)
                            K.barrier()
                        with ExitStack() as ew:
                            slab = [K.sb([128, NF, 256], BF16, ew) for _ in range(2)]
                            Fm = [K.sb([128, 256], F32, ew) for _ in range(2)]
                            wdv = w_down.t.rearrange("(f p) n -> p f n", p=128)
                            for cg in range(8):
                                sl_ = slab[cg % 2]
                                cs_ = slice(cg * 256, cg * 256 + 256)
                                dma(POOL, sl_[:, :, :], V(wdv[:, :, cs_], w_down.k))
                                for t in range(4):
                                    P = PS[(cg * 4 + t) % 4]
                                    for f in range(NF):
                                        mm(P[:, 0:256], ffT[:, f, t * 128:(t + 1) * 128], sl_[:, f, :],
                                           start=(f == 0), stop=(f == NF - 1), defer=(f != NF - 1))
                                    f_ = Fm[t % 2]
                                    tt(DVE, f_[:, :], P[:, 0:256], g2bc[:, cs_], ALU.mult)
                                    tt(DVE, h1[t][:, cs_], f_[:, :], h1[t][:, cs_], ALU.add)
                            K.barrier()
                    for t in range(4):
                        r0 = half * 512 + t * 128
                        dma(SP, out_d[r0:r0 + 128, :], h1[t][:, :], nowaw=True)
                    K.barrier()
                    for sid, val in out_d.k.w.items():
                        SP._wait(sid, val)
    return nc


def _consts():
    i = np.arange(128)
    same = (i[:, None] // 64) == (i[None, :] // 64)
    ident = np.eye(128, dtype=np.float32)
    tri2 = ((i[:, None] <= i[None, :]) & same).astype(np.float32)
    bones = same.astype(np.float32)
    mLs = ((i[:, None] > i[None, :]) & same).astype(np.float32)
    mUi = ((i[None, :] >= i[:, None]) & same).astype(np.float32)
    negU = -(i[:, None] >= i[None, :]).astype(np.float32)
    tq = np.arange(512)
    md = [((r * 128 + i[:, None]) < tq[None, :]).astype(np.float32) for r in range(4)]
    sel = np.stack([(i // 64) == 0, (i // 64) == 1], axis=1).astype(np.float32)
    return np.ascontiguousarray(np.concatenate([ident, tri2, bones, mLs, mUi, negU] + md + [sel], axis=1))


_NC_CACHE = {}


def kernel(x, c, w_mod, b_mod, norm1_w, w_in, q_norm_w, k_norm_w, conv_w, a_log, dt_bias,
           o_norm_w, p_a, p_b, w_out, norm2_w, w_gate, w_up, w_down):
    f = lambda a: np.ascontiguousarray(np.asarray(a, dtype=np.float32))
    x = f(x)
    SEQ = x.shape[1]
    TOWN = SEQ // NCORE
    xa = x[0]
    col16 = lambda v: f(np.asarray(v).reshape(NK, 128).T)
    w_in0 = f(w_in)[0]
    conv0 = f(conv_w)[0]
    wm0 = f(w_mod)[0]
    bm0 = f(np.asarray(b_mod)[0].reshape(96, 128).T)
    shared = {
        "x_all": xa,
        "c_col": col16(np.asarray(c)[0]),

        "n1w_col": col16(np.asarray(norm1_w)[0]),
        "n2w_col": col16(np.asarray(norm2_w)[0]),
        "w_g": f(w_in0[:, 14368:18464]),
        "qnw_col": f(np.asarray(q_norm_w)[0].reshape(128, 1)),
        "knw_col": f(np.asarray(k_norm_w)[0].reshape(128, 1)),
        "onw_bc": f(np.broadcast_to(np.asarray(o_norm_w)[0][None, :], (128, 128))),
        "p_a": f(p_a)[0], "p_b": f(p_b)[0], "w_out": f(w_out)[0],
        "w_gate": f(w_gate)[0], "w_up": f(w_up)[0], "w_down": f(w_down)[0],
        "cst": _consts(),
    }
    if STOP is not None:
        for k_ in ("w_g", "p_a", "p_b", "w_out", "w_gate", "w_up", "w_down"):
            shared.pop(k_)
    in_maps = []
    for core in range(NCORE):
        hs = [2 * core, 2 * core + 1]
        cols = []
        for base in (0, 2048):
            for h in hs:
                cols.append(np.arange(base + h * 128, base + (h + 1) * 128))
        for base in (6144, 6144 + 2048, 6144 + 4096):
            for h in hs:
                cols.append(np.arange(base + h * 128, base + (h + 1) * 128))
        for base in (4096, 12288):
            for h in hs:
                cols.append(np.arange(base + h * 128, base + (h + 1) * 128))
        cols.append(np.array([14336 + hs[0], 14336 + hs[1], 14352 + hs[0], 14352 + hs[1]]))
        cols = np.concatenate(cols)
        cw = np.zeros((128, 24), np.float32)
        for hi, h in enumerate(hs):
            for typ in range(3):
                blk = conv0[:, typ * 2048 + h * 128: typ * 2048 + (h + 1) * 128]
                cw[:, (hi * 3 + typ) * 4:(hi * 3 + typ) * 4 + 4] = blk.T
        m = dict(shared)
        m["x_own"] = np.ascontiguousarray(xa[core * TOWN:(core + 1) * TOWN])
        m["w_mod"] = np.ascontiguousarray(wm0[:, core * 1536:(core + 1) * 1536])
        m["bmod_col"] = np.ascontiguousarray(bm0[:, core * 12:(core + 1) * 12])
        m["w_h"] = np.ascontiguousarray(w_in0[:, cols])
        m["convw"] = cw
        m["alog_bc"] = f(np.broadcast_to(np.asarray(a_log)[0][hs][None, :], (128, 2)))
        m["dtb_bc"] = f(np.broadcast_to(np.asarray(dt_bias)[0][hs][None, :], (128, 2)))
        in_maps.append(m)
    if SEQ not in _NC_CACHE:
        _NC_CACHE[SEQ] = build(SEQ)
    nc = _NC_CACHE[SEQ]
    res = run_bass_kernel_spmd(nc, in_maps, core_ids=list(range(NCORE)))
    out = np.concatenate([np.asarray(r["out"]) for r in res.results], axis=0)
    return out.reshape(1, SEQ, D).astype(np.float32)
```

```python
import numpy as np
from contextlib import ExitStack
import concourse.bass as bass
import concourse.mybir as mybir
from concourse.bass_utils import run_bass_kernel_spmd

F32 = mybir.dt.float32
BF16 = mybir.dt.bfloat16
AF = mybir.ActivationFunctionType
ALU = mybir.AluOpType

D = 2048
NK = 16
DFF = 5632
NF = 44
EPS = 1e-6
NCORE = 8
WH = 1796


_RECORD = False
_NEEDED = {}
_RANK = {}


class Trk:
    __slots__ = ("w", "r", "dsem", "dval", "dq")

    def __init__(s):
        s.w = {}
        s.r = {}
        s.dsem = None
        s.dval = 0
        s.dq = None


class V:
    __slots__ = ("ap", "k")

    def __init__(s, ap, k):
        s.ap = ap
        s.k = k


class TT:
    def __init__(s, t, k=None):
        s.t = t
        s.k = k if k is not None else Trk()

    def __getitem__(s, idx):
        return V(s.t[idx], s.k)


class Eng:
    def __init__(s, K, name, eng, same):
        s.K = K
        s.name = name
        s.eng = eng
        s.same = same
        s.own = set()
        s.sid = K.newsem(name)
        K.tl[s.sid] = name
        s.own.add(s.sid)
        s.cnt = 0
        s.last = None
        s.seen = {}
        s.pend_r = []
        s.pend_w = []

    def _wait(s, sid, val):
        if s.seen.get(sid, 0) >= val:
            return
        s.seen[sid] = val
        nm = s.K.tl.get(sid)
        if nm is not None:
            if _RECORD:
                _NEEDED.setdefault(nm, set()).add(val)
            else:
                val = _RANK[nm][val]
        s.eng.wait_ge(s.K.h[sid], val)

    def issue(s, fn, r, w, dma_k=None, nowaw=False, defer=False):
        need = {}
        for t in r:
            for sid, val in t.w.items():
                if sid in s.own and not s.same:
                    continue
                if need.get(sid, 0) < val:
                    need[sid] = val
        for t in w:
            if not nowaw:
                for sid, val in t.w.items():
                    if sid in s.own and not s.same:
                        continue
                    if need.get(sid, 0) < val:
                        need[sid] = val
            for sid, val in t.r.items():
                if sid in s.own and not s.same:
                    continue
                if need.get(sid, 0) < val:
                    need[sid] = val
        for sid, val in need.items():
            s._wait(sid, val)
        ins = fn()
        if dma_k is not None:
            k = dma_k
            s.K.all_dma.add(k)
            if k.dsem is None:
                fl = s.K.free_dsems.setdefault(s.name, [])
                k.dq = s.name
                if fl:
                    k.dsem, k.dval = fl.pop()
                else:
                    k.dsem = s.K.newsem("d")
            assert k.dq == s.name, (k.dq, s.name)
            k.dval += 16
            ins.then_inc(s.K.h[k.dsem], 16)
            tok = (k.dsem, k.dval)
        else:
            if defer:
                s.pend_r += r
                s.pend_w += w
                return
            s.cnt += 1
            if _RECORD or s.cnt in _RANK.get(s.name, {}):
                ins.then_inc(s.K.h[s.sid], 1)
            tok = (s.sid, s.cnt)
            s.last = tok
            r = r + s.pend_r
            w = w + s.pend_w
            s.pend_r = []
            s.pend_w = []
        for t in r:
            if t.r.get(tok[0], 0) < tok[1]:
                t.r[tok[0]] = tok[1]
        for t in w:
            if nowaw:
                t.w[tok[0]] = tok[1]
            else:
                t.w = {tok[0]: tok[1]}
            t.r = {}


class Kern:
    def __init__(s, nc, es):
        s.nc = nc
        s.es = es
        s.h = []
        s.uid = 0
        s.free_dsems = {}
        s.all_dma = set()
        s.tl = {}
        s.PE = Eng(s, "pe", nc.tensor, False)
        s.ACT = Eng(s, "act", nc.scalar, True)
        s.DVE = Eng(s, "dve", nc.vector, True)
        s.POOL = Eng(s, "pool", nc.gpsimd, True)
        s.SP = Eng(s, "sp", nc.sync, True)
        s.engs = [s.PE, s.ACT, s.DVE, s.POOL, s.SP]

    def newsem(s, name):
        s.uid += 1
        sem = s.es.enter_context(s.nc.semaphore(f"{name}{s.uid}"))
        s.h.append(sem)
        return len(s.h) - 1

    def sb(s, shape, dt, es=None, name="t"):
        s.uid += 1
        t = (es or s.es).enter_context(s.nc.sbuf_tensor(f"{name}{s.uid}", list(shape), dt))
        o = TT(t)
        if es is not None:
            def _rel(k=o.k, K=s):
                if k.dsem is not None:
                    K.free_dsems.setdefault(k.dq, []).append((k.dsem, k.dval))
                    k.dsem = None
            es.callback(_rel)
        return o

    def barrier(s):
        for e in s.engs:
            for o in s.engs:
                if o is not e and o.last is not None:
                    e._wait(o.last[0], o.last[1])

    def do(s, E, meth, outs, ins, nowaw=False, defer=False, **kw):
        r = [v.k for v in ins.values() if isinstance(v, V)]
        w = [v.k for v in outs.values()]
        args = {}
        for k_, v in list(outs.items()) + list(ins.items()):
            args[k_] = v.ap if isinstance(v, V) else v
        args.update(kw)
        E.issue(lambda: getattr(E.eng, meth)(**args), r, w, nowaw=nowaw, defer=defer)

    def mm(s, out, lhsT, rhs, start=True, stop=True, defer=False):
        s.do(s.PE, "matmul", dict(out=out), dict(lhsT=lhsT, rhs=rhs), start=start, stop=stop, defer=defer)

    def tr(s, out, in_, ident, defer=False):
        s.do(s.PE, "transpose", dict(out=out), dict(in_=in_, identity=ident), defer=defer)

    def act(s, out, in_, func, scale=1.0, bias=0.0, accum=None):
        outs = dict(out=out)
        if accum is not None:
            outs["accum_out"] = accum
        s.do(s.ACT, "activation", outs, dict(in_=in_, bias=bias, scale=scale), func=func)

    def ts(s, E, out, in0, s1, s2, op0, op1=None):
        kw = dict(op0=op0)
        if op1 is not None:
            kw["op1"] = op1
        s.do(E, "tensor_scalar", dict(out=out), dict(in0=in0, scalar1=s1, scalar2=s2), **kw)

    def stt(s, out, in0, scalar, in1, op0, op1):
        s.do(s.DVE, "scalar_tensor_tensor", dict(out=out), dict(in0=in0, scalar=scalar, in1=in1), op0=op0, op1=op1)

    def tt(s, E, out, in0, in1, op):
        s.do(E, "tensor_tensor", dict(out=out), dict(in0=in0, in1=in1), op=op)

    def copy(s, E, out, in_):
        if E is s.ACT:
            s.act(out, in_, AF.Identity)
        else:
            s.do(E, "tensor_copy", dict(out=out), dict(in_=in_))

    def memset(s, E, out, val):
        s.do(E, "memset", dict(ap=out), {}, constant=val)

    def dma(s, E, out, in_, nowaw=False):
        r = [in_.k]
        w = [out.k]
        E.issue(lambda: E.eng.dma_start(out=out.ap, in_=in_.ap), r, w, dma_k=out.k, nowaw=nowaw)


class _Stop(Exception):
    pass


import os
STOP = os.environ.get("KSTOP") or None
SIMDBG = False
ATT_SKEW = 1
GDN_MODE = 1


def build(SEQ=8192):
    global _RECORD, _NEEDED, _RANK
    _RECORD, _NEEDED, _RANK = True, {}, {}
    try:
        _build(SEQ)
    except _Stop:
        pass
    _RANK = {nm: {v: i + 1 for i, v in enumerate(sorted(vs))} for nm, vs in _NEEDED.items()}
    _RECORD = False
    try:
        return _build(SEQ)
    except _Stop as e:
        return e.args[0]


def _build(SEQ=8192):
    NT = SEQ // 128
    NG = SEQ // 512
    TOWN = SEQ // NCORE
    NHALF = TOWN // 512
    nc = bass.Bass("TRN2", target_bir_lowering=False)

    def din(name, shape):
        return TT(nc.dram_tensor(name, list(shape), F32, kind="ExternalInput").ap())

    x_all = din("x_all", [SEQ, D])
    x_own = din("x_own", [TOWN, D])
    c_col = din("c_col", [128, NK])
    w_mod = din("w_mod", [D, 1536])
    bmod_col = din("bmod_col", [128, 12])
    msend = TT(nc.dram_tensor("msend", [128, 64], F32).ap())
    mrecv = TT(nc.dram_tensor("mrecv", [8 * 128, 64], F32).ap())
    n1w = din("n1w_col", [128, NK])
    n2w = din("n2w_col", [128, NK])
    w_h = din("w_h", [D, WH])
    qnw = din("qnw_col", [128, 1])
    knw = din("knw_col", [128, 1])
    convw = din("convw", [128, 24])
    alog_bc = din("alog_bc", [128, 2])
    dtb_bc = din("dtb_bc", [128, 2])
    onw_bc = din("onw_bc", [128, 128])
    if STOP is None:
        w_g = din("w_g", [D, 2 * D])
        p_a = din("p_a", [D, D])
        p_b = din("p_b", [D, D])
        w_out = din("w_out", [D, D])
        w_gate = din("w_gate", [D, DFF])
        w_up = din("w_up", [D, DFF])
        w_down = din("w_down", [DFF, D])
    NCST = 6 * 128 + 4 * 512 + 2
    cst = din("cst", [128, NCST])
    out_d = TT(nc.dram_tensor("out", [TOWN, D], F32, kind="ExternalOutput").ap())
    if SIMDBG:
        modc_dbg = din("modc_dbg", [128, 96])
        dbg = TT(nc.dram_tensor("dbg", [128, 8192], F32, kind="ExternalOutput").ap())
    send = TT(nc.dram_tensor("sendb", [8 * 512, TOWN], BF16).ap())
    recv = TT(nc.dram_tensor("recvb", [8 * 8 * 512, TOWN], BF16).ap())

    with ExitStack() as es:
        K = Kern(nc, es)
        PE, ACT, DVE, POOL, SP = K.PE, K.ACT, K.DVE, K.POOL, K.SP

        def dump(v, c0, n):
            if SIMDBG:
                POOL.issue(lambda: nc.gpsimd.dma_start(out=dbg.t[:, c0:c0 + n], in_=v.ap), [v.k], [dbg.k],
                           dma_k=dbg.k, nowaw=True)

        def chk(name):
            if STOP == name:
                K.barrier()
                for k_ in K.all_dma:
                    if k_.dsem is not None:
                        (POOL if k_.dq == "pool" else SP)._wait(k_.dsem, k_.dval)
                for q_, fl_ in K.free_dsems.items():
                    for sid_, val_ in fl_:
                        (POOL if q_ == "pool" else SP)._wait(sid_, val_)
                print("STOP at", name, "sems", len(K.h), [(e.name, e.cnt) for e in K.engs])
                raise _Stop(nc)
        mm, tr, act, ts, stt, tt, copy, memset, dma = K.mm, K.tr, K.act, K.ts, K.stt, K.tt, K.copy, K.memset, K.dma

        PSt = [es.enter_context(nc.psum_tensor(f"ps{i}", [128, 512], F32)) for i in range(7)]
        PS = [TT(t) for t in PSt]
        PSBt = es.enter_context(nc.psum_tensor("psb", [128, 1024], BF16))
        _pk = Trk()
        PSB = [TT(PSBt, _pk), TT(PSBt, _pk)]

        def pslot(i, c0, c1, p0=0, p1=128):
            class _S:
                pass
            o = _S()
            o.k = PS[i].k

            def gi(idx, _i=i, _c0=c0):
                ps, cs = idx
                a = _c0 + (cs.start or 0)
                b = _c0 + (cs.stop if cs.stop is not None else (c1 - c0))
                return V(PSt[_i][ps, a:b], o.k)
            o.g = gi
            return o

        ident_f = K.sb([128, 128], F32)
        ident_b = K.sb([128, 128], BF16)
        ones_f = K.sb([128, 128], F32)
        nones_f = K.sb([128, 128], F32)
        ones_b = K.sb([128, 128], BF16)
        nones_b = K.sb([128, 128], BF16)
        tri2 = K.sb([128, 128], F32)
        bones = K.sb([128, 128], F32)
        mLs = K.sb([128, 128], F32)
        mUi = K.sb([128, 128], F32)
        negU = K.sb([128, 128], BF16)
        mdiag = K.sb([128, 4, 512], BF16)
        sel = K.sb([128, 2], F32)
        with ExitStack() as e0:
            cs_t = K.sb([128, NCST], F32, e0)
            dma(SP, cs_t[:, :], cst[:, :])
            copy(DVE, ident_f[:, :], cs_t[:, 0:128])
            copy(DVE, ident_b[:, :], cs_t[:, 0:128])
            copy(DVE, tri2[:, :], cs_t[:, 128:256])
            copy(DVE, bones[:, :], cs_t[:, 256:384])
            copy(DVE, mLs[:, :], cs_t[:, 384:512])
            copy(DVE, mUi[:, :], cs_t[:, 512:640])
            copy(DVE, negU[:, :], cs_t[:, 640:768])
            for r_ in range(4):
                copy(DVE, mdiag[:, r_, :], cs_t[:, 768 + 512 * r_:768 + 512 * (r_ + 1)])
            copy(DVE, sel[:, :], cs_t[:, 768 + 2048:768 + 2050])
            memset(DVE, ones_f[:, :], 1.0)
            memset(DVE, nones_f[:, :], -1.0)
            memset(DVE, ones_b[:, :], 1.0)
            memset(DVE, nones_b[:, :], -1.0)
            K.barrier()

        modc = K.sb([128, 96], F32)
        a1 = K.sb([128, NK], F32)
        a2 = K.sb([128, NK], F32)
        small = K.sb([128, 64], F32)
        with ExitStack() as e0:
            cc = K.sb([128, NK], F32, e0)
            cb = K.sb([128, NK], BF16, e0)
            bm = K.sb([128, 12], F32, e0)
            modl = K.sb([128, 64], F32, e0)
            memset(DVE, modl[:, :], 0.0)
            nw1 = K.sb([128, NK], F32, e0)
            nw2 = K.sb([128, NK], F32, e0)
            slabs = [K.sb([128, NK, 512], BF16, e0) for _ in range(3)]
            dma(SP, cc[:, :], c_col[:, :])
            if not SIMDBG:
                dma(SP, bm[:, :], bmod_col[:, :])
            dma(SP, nw1[:, :], n1w[:, :])
            dma(SP, nw2[:, :], n2w[:, :])
            act(cb[:, :], cc[:, :], AF.Silu)
            wm = w_mod.t.rearrange("(k p) n -> p k n", p=128)
            pm = PS[0]
            for sl in range(0 if SIMDBG else 3):
                S_ = slabs[sl % 3]
                dma(POOL, S_[:, :, :], V(wm[:, :, sl * 512:(sl + 1) * 512], w_mod.k))
                for jb in range(4):
                    j = sl * 4 + jb
                    for k in range(NK):
                        mm(pm[:, j:j + 1], S_[:, k, jb * 128:(jb + 1) * 128], cb[:, k:k + 1],
                           start=(k == 0), stop=(k == NK - 1), defer=(k != NK - 1))
            if SIMDBG:
                dma(SP, modc[:, :], modc_dbg[:, :])
            else:
              tt(DVE, modl[:, 0:12], pm[:, 0:12], bm[:, :], ALU.add)
              dma(SP, msend[:, :], modl[:, :])
              POOL.issue(lambda: nc.gpsimd.collective_compute(
                "AllGather", ALU.bypass, replica_groups=[list(range(NCORE))], ins=[msend.t], outs=[mrecv.t]),
                [msend.k], [mrecv.k])
              POOL._wait(POOL.sid, POOL.cnt)
              dma(SP, V(modc.t[:, :].rearrange("p (r j) -> p r j", r=8), modc.k),
                  V(mrecv.t.rearrange("(r p) j -> p r j", p=128)[:, :, 0:12], mrecv.k))
            stt(a1[:, :], modc[:, 16:32], 1.0, nw1[:, :], ALU.add, ALU.mult)
            stt(a2[:, :], modc[:, 64:80], 1.0, nw2[:, :], ALU.add, ALU.mult)
            K.barrier()
        chk("p0")
        b1 = lambda k: modc[:, k:k + 1]
        b2 = lambda k: modc[:, 48 + k:49 + k]

        def make_uT(es_, load_tile, n_tiles, acol, bcol, uT):
            xts = [K.sb([128, D], F32, es_) for _ in range(2)]
            xns = [K.sb([128, D], BF16, es_) for _ in range(4)]
            st = K.sb([128, 8], F32, es_)
            for t in range(n_tiles):
                xt = load_tile(t, xts[t % 2])
                xn = xns[t % 4]
                act(xn[:, :], xt[:, :], AF.Square, accum=st[:, 0:1])
                ts(DVE, st[:, 1:2], st[:, 0:1], 1.0 / D, EPS, ALU.mult, ALU.add)
                act(st[:, 2:3], st[:, 1:2], AF.Ln)
                act(st[:, 3:4], st[:, 2:3], AF.Exp, scale=-0.5)
                ts(DVE, xn[:, :], xt[:, :], st[:, 3:4], None, ALU.mult)
            for k in range(NK):
                pb = PSB[k % 2]
                c0 = (k % 2) * 512
                for t in range(n_tiles):
                    tr(V(PSBt[:, c0 + t * 128:c0 + (t + 1) * 128], pb.k), xns[t % 4][:, k * 128:(k + 1) * 128],
                       ident_b[:, :], defer=(t != n_tiles - 1))
                src = V(PSBt[:, c0:c0 + n_tiles * 128], pb.k)
                if k % 2 == 0:
                    ts(DVE, uT[:, k, :], src, acol[:, k:k + 1], bcol(k), ALU.mult, ALU.add)
                else:
                    act(uT[:, k, :], src, AF.Identity, scale=acol[:, k:k + 1], bias=bcol(k))

        with ExitStack() as e1:
            Wh = K.sb([128, NK, WH], BF16, e1)
            whv = w_h.t.rearrange("(k p) n -> p k n", p=128)
            for k in range(NK):
                dma(POOL, Wh[:, k, :], V(whv[:, k, :], w_h.k), nowaw=True)
            KT = [K.sb([128, SEQ], BF16, e1) for _ in range(2)]
            VA = [K.sb([128, NT, 128], BF16, e1) for _ in range(2)]
            uT = K.sb([128, NK, 512], BF16, e1)
            qT = [K.sb([128, 512], BF16, e1) for _ in range(2)]
            qbT = [K.sb([128, 512], BF16, e1) for _ in range(2)]
            kbT = [K.sb([128, 512], BF16, e1) for _ in range(2)]
            vbT = [K.sb([128, 512], BF16, e1) for _ in range(2)]
            hist = [K.sb([128, 4], F32, e1) for _ in range(6)]
            siluz = K.sb([128, 4, 256], BF16, e1)
            betag = K.sb([128, 4, 4], F32, e1)
            stage = [K.sb([128, 512], BF16, e1) for _ in range(4)]
            Sf = [K.sb([128, 128], F32, e1) for _ in range(2)]
            Sb = [K.sb([128, 128], BF16, e1) for _ in range(2)]
            cw = K.sb([128, 24], F32, e1)
            qcol = K.sb([128, 4], F32, e1)
            negea = K.sb([128, 2], F32, e1)
            dtb = K.sb([128, 2], F32, e1)
            onw = K.sb([128, 128], F32, e1)
            dma(SP, cw[:, :], convw[:, :])
            dma(SP, qcol[:, 2:3], qnw[:, :])
            dma(SP, qcol[:, 1:2], knw[:, :])
            dma(SP, negea[:, :], alog_bc[:, :])
            dma(SP, dtb[:, :], dtb_bc[:, :])
            dma(SP, onw[:, :], onw_bc[:, :])
            ts(DVE, qcol[:, 0:1], qcol[:, 2:3], float(128 ** -0.5), None, ALU.mult)
            act(negea[:, :], negea[:, :], AF.Exp)
            ts(DVE, negea[:, :], negea[:, :], -1.0, None, ALU.mult)
            for h_ in hist:
                memset(DVE, h_[:, :], 0.0)
            for h in range(2):
                memset(DVE, Sf[h][:, :], 0.0)
                memset(DVE, Sb[h][:, :], 0.0)
            K.barrier()
            sendv = send.t.rearrange("(t a h p) c -> t a h p c", t=8, a=2, h=2)

            for g in range(NG):
                with ExitStack() as ea:
                    def load_x(t, dst, _g=g):
                        ti = _g * 4 + t
                        dma(SP, dst[:, :], x_all[ti * 128:(ti + 1) * 128, :])
                        return dst
                    make_uT(ea, load_x, 4, a1, b1, uT)
                    if g == 0:
                        dump(uT[:, 0, :], 0, 512)
                        dump(uT[:, 5, :], 512, 512)
                    K.barrier()
                    chk("1a")
                with ExitStack() as eb:
                    F = [K.sb([128, 512], F32, eb) for _ in range(4)]
                    raw = [K.sb([128, 516], F32, eb) for _ in range(2)]
                    sqb = [K.sb([128, 512], BF16, eb) for _ in range(2)]
                    sm = K.sb([128, 16], F32, eb)
                    for blk in range(10):
                        P = PS[blk % 4]
                        for k in range(NK):
                            mm(P[:, :], Wh[:, k, blk * 128:(blk + 1) * 128], uT[:, k, :],
                               start=(k == 0), stop=(k == NK - 1), defer=(k != NK - 1))
                        if blk < 4:
                            h = blk % 2
                            isq = blk < 2
                            sq = sqb[blk % 2]
                            act(sq[:, :], P[:, :], AF.Square)
                            Pss = PS[4 + blk % 2]
                            mm(Pss[:, :], ones_b[:, :], sq[:, :])
                            lnv = F[0]
                            rs = F[1]
                            ts(DVE, lnv[:, :], Pss[:, :], 1.0 / 128, EPS, ALU.mult, ALU.add)
                            act(lnv[:, :], lnv[:, :], AF.Ln)
                            act(rs[:, :], lnv[:, :], AF.Exp, scale=-0.5)
                            dst = qT[h][:, :] if isq else KT[h][:, g * 512:(g + 1) * 512]
                            stt(dst, P[:, :], qcol[:, 0:1] if isq else qcol[:, 1:2], rs[:, :], ALU.mult, ALU.mult)
                        else:
                            s_ = blk - 4
                            typ = s_ // 2
                            h = s_ % 2
                            rw = raw[s_ % 2]
                            copy(ACT, rw[:, 3:515], P[:, :])
                            copy(DVE, rw[:, 0:3], hist[s_][:, 0:3])
                            acc = F[2]
                            ci = (h * 3 + typ) * 4
                            ts(DVE, acc[:, :], rw[:, 0:512], cw[:, ci:ci + 1], None, ALU.mult)
                            for j in range(1, 4):
                                stt(acc[:, :], rw[:, j:j + 512], cw[:, ci + j:ci + j + 1], acc[:, :], ALU.mult, ALU.add)
                            copy(DVE, hist[s_][:, 0:3], rw[:, 512:515])
                            if typ == 2:
                                act(vbT[h][:, :], acc[:, :], AF.Silu)
                            else:
                                slt = F[3]
                                act(slt[:, :], acc[:, :], AF.Silu)
                                sq = sqb[blk % 2]
                                act(sq[:, :], slt[:, :], AF.Square)
                                Pss = PS[4 + blk % 2]
                                mm(Pss[:, :], ones_b[:, :], sq[:, :])
                                lnv = F[0]
                                rs = F[1]
                                ts(DVE, lnv[:, :], Pss[:, :], EPS, None, ALU.add)
                                act(lnv[:, :], lnv[:, :], AF.Ln)
                                act(rs[:, :], lnv[:, :], AF.Exp, scale=-0.5)
                                dst = qbT[h] if typ == 0 else kbT[h]
                                stt(dst[:, :], slt[:, :], float(128 ** -0.5) if typ == 0 else 1.0, rs[:, :],
                                    ALU.mult, ALU.mult)
                    for t in range(4):
                        ti = g * 4 + t
                        P1 = PS[t % 2]
                        P2 = PS[2 + t % 2]
                        for k in range(NK):
                            mm(P1[:, :], uT[:, k, t * 128:(t + 1) * 128], Wh[:, k, 1280:1792],
                               start=(k == 0), stop=(k == NK - 1), defer=True)
                        for k in range(NK):
                            mm(P2[:, 0:4], uT[:, k, t * 128:(t + 1) * 128], Wh[:, k, 1792:1796],
                               start=(k == 0), stop=(k == NK - 1), defer=(k != NK - 1))
                        copy(ACT, VA[0][:, ti, :], P1[:, 0:128])
                        copy(DVE, VA[1][:, ti, :], P1[:, 128:256])
                        act(siluz[:, t, :], P1[:, 256:512], AF.Silu)
                        act(betag[:, t, 0:2], P2[:, 0:2], AF.Sigmoid)
                        tt(DVE, sm[:, 0:2], P2[:, 2:4], dtb[:, :], ALU.add)
                        act(sm[:, 2:4], sm[:, 0:2], AF.Exp)
                        act(sm[:, 4:6], sm[:, 2:4], AF.Ln, bias=1.0)
                        tt(DVE, betag[:, t, 2:4], sm[:, 4:6], negea[:, :], ALU.mult)
                    if g == 0:
                        dump(qT[1][:, :], 1024, 512)
                        dump(KT[0][:, 0:512], 1536, 512)
                        dump(qbT[0][:, :], 2048, 512)
                        dump(kbT[1][:, :], 2560, 512)
                        dump(vbT[0][:, :], 3072, 512)
                        dump(VA[1][:, 1, :], 3584, 128)
                        dump(siluz[:, 2, :], 3712, 256)
                        dump(V(betag.t[:, :, :].rearrange("p a b -> p (a b)"), betag.k), 3968, 16)
                    K.barrier()
                    chk("1b")
                with ExitStack() as ec:
                    Ebuf = [K.sb([128, 512], F32, ec) for _ in range(2)]
                    SPb = [K.sb([128, 512], BF16, ec) for _ in range(2)]
                    tmpb = [K.sb([128, 512], F32, ec) for _ in range(2)]
                    Wb = [K.sb([128, 512], BF16, ec) for _ in range(2)]
                    Rbc = K.sb([128, 512], F32, ec)
                    for h in range(2):
                        memset(DVE, Rbc[:, :], 0.0)
                        PO = PS[6]
                        js = list(range(4 * g + 3, -1, -1))
                        nj = len(js)

                        def stA(idx, h=h, js=js):
                            j = js[idx]
                            b = idx % 2
                            r_ = j - 4 * g
                            kt = KT[h][:, j * 128:(j + 1) * 128]
                            mm(PS[b][:, :], kt, qT[h][:, :])
                            act(Ebuf[b][:, :], PS[b][:, :], AF.Exp)
                            act(SPb[b][:, :], Ebuf[b][:, :], AF.Ln, bias=1.0)
                            if r_ >= 0:
                                tt(DVE, SPb[b][:, :], SPb[b][:, :], mdiag[:, r_, :], ALU.mult)

                        def stB(idx, h=h, js=js):
                            j = js[idx]
                            b = idx % 2
                            r_ = j - 4 * g
                            PA, PB = PS[2 + b], PS[4 + b]
                            kt = KT[h][:, j * 128:(j + 1) * 128]
                            S_ = SPb[b]
                            mm(PA[:, :], kt, qT[h][:, :], start=True, stop=False, defer=True)
                            mm(PA[:, :], negU[:, :], S_[:, :], start=False, stop=True)
                            mm(PB[:, :], nones_b[:, :], S_[:, :])
                            T_ = tmpb[b]
                            tt(DVE, T_[:, :], PA[:, :], Rbc[:, :], ALU.add)
                            tt(DVE, Rbc[:, :], PB[:, :], Rbc[:, :], ALU.add)
                            W_ = Wb[b]
                            act(W_[:, :], T_[:, :], AF.Exp)
                            if r_ >= 0:
                                tt(DVE, W_[:, :], W_[:, :], mdiag[:, r_, :], ALU.mult)

                        def stC(idx, h=h, js=js, nj=nj):
                            j = js[idx]
                            b = idx % 2
                            mm(PO[:, :], VA[h][:, j, :], Wb[b][:, :], start=(idx == 0), stop=(idx == nj - 1))

                        if ATT_SKEW:
                            for s_ in range(nj + 2):
                                if s_ < nj:
                                    stA(s_)
                                if 0 <= s_ - 1 < nj:
                                    stB(s_ - 1)
                                if 0 <= s_ - 2 < nj:
                                    stC(s_ - 2)
                        else:
                            for s_ in range(nj):
                                stA(s_)
                                stB(s_)
                                stC(s_)
                        copy(ACT, stage[h][:, :], PO[:, :])
                        dma(SP, V(sendv[(g * 512) // TOWN, 0, h, :, (g * 512) % TOWN:(g * 512) % TOWN + 512], send.k),
                            stage[h][:, :], nowaw=True)
                    if g == 0:
                        dump(stage[0][:, :], 4096, 512)
                        dump(stage[1][:, :], 4608, 512)
                    K.barrier()
                    chk("1c")
                with ExitStack() as ed:
                    def f128(n, dt=F32):
                        return [K.sb([128, 128], dt, ed) for _ in range(n)]
                    G1, dl, du, Lm, Um, Ym, wv, tq, oraw, on_ = (f128(2) for _ in range(10))
                    PA_, PB_ = f128(4), f128(4)
                    attnT, Yb, wkT, kbt, kdec, vbt, ub, obb = (f128(2, BF16) for _ in range(8))
                    sc = K.sb([128, 32], F32, ed)
                    Pg = pslot(0, 0, 16)
                    SL = []
                    for h in range(2):
                        bank = 1 + h * 3
                        SL.append(dict(
                            Pdiff=pslot(bank, 0, 128), Pkk=pslot(bank, 128, 256), Pqk=pslot(bank, 256, 384),
                            PU=pslot(bank, 384, 512),
                            NP=[pslot(bank + 1, i * 128, (i + 1) * 128) for i in range(4)],
                            CP=[pslot(bank + 2, i * 128, (i + 1) * 128) for i in range(4)]))
                    for t in range(4):
                        ti = g * 4 + t
                        cols = slice(t * 128, (t + 1) * 128)
                        gsel = sc[:, 0:4]
                        for c_ in range(2):
                            ts(DVE, sc[:, 2 * c_:2 * c_ + 2], betag[:, t, 2:4], sel[:, c_:c_ + 1], None, ALU.mult)
                        mm(Pg.g((slice(None), slice(0, 2))), tri2[:, :], betag[:, t, 2:4])
                        mm(Pg.g((slice(None), slice(2, 4))), bones[:, :], betag[:, t, 2:4])
                        mm(Pg.g((slice(None), slice(4, 8))), ones_f[:, :], gsel)
                        gcs = sc[:, 4:12]
                        copy(ACT, gcs, Pg.g((slice(None), slice(0, 8))))
                        act(sc[:, 12:14], sc[:, 4:6], AF.Exp)
                        tt(DVE, sc[:, 14:16], sc[:, 6:8], sc[:, 4:6], ALU.subtract)
                        act(sc[:, 14:16], sc[:, 14:16], AF.Exp)
                        act(sc[:, 16:20], sc[:, 8:12], AF.Exp)
                        tt(DVE, sc[:, 20:22], betag[:, t, 0:2], sc[:, 12:14], ALU.mult)
                        if g == 0 and t == 0:
                            chk("d1")
                        def gdn_head(h, t=t, cols=cols, ti=ti):
                            beta_c = betag[:, t, h:h + 1]
                            g_c = betag[:, t, 2 + h:3 + h]
                            egc_c = sc[:, 12 + h:13 + h]
                            ekd_c = sc[:, 14 + h:15 + h]
                            bege_c = sc[:, 20 + h:21 + h]
                            Pdiff, Pkk, Pqk, PU = SL[h]["Pdiff"], SL[h]["Pkk"], SL[h]["Pqk"], SL[h]["PU"]
                            NP, CP = SL[h]["NP"], SL[h]["CP"]
                            A_ = (slice(None), slice(0, 128))
                            kT_t = kbT[h][:, cols]
                            qT_t = qbT[h][:, cols]
                            ts(DVE, G1[h][:, :], tri2[:, :], g_c, None, ALU.mult)
                            mm(Pdiff.g(A_), G1[h][:, :], ones_f[:, :], start=True, stop=False, defer=True)
                            mm(Pdiff.g(A_), nones_f[:, :], G1[h][:, :], start=False, stop=True)
                            mm(Pkk.g(A_), kT_t, kT_t)
                            mm(Pqk.g(A_), kT_t, qT_t)
                            yield
                            ts(DVE, dl[h][:, :], Pdiff.g(A_), 0.0, None, ALU.min)
                            ts(DVE, du[h][:, :], Pdiff.g(A_), -1.0, 0.0, ALU.mult, ALU.min)
                            act(dl[h][:, :], dl[h][:, :], AF.Exp)
                            act(du[h][:, :], du[h][:, :], AF.Exp)
                            yield
                            tt(DVE, dl[h][:, :], dl[h][:, :], mLs[:, :], ALU.mult)
                            tt(DVE, du[h][:, :], du[h][:, :], mUi[:, :], ALU.mult)
                            stt(Lm[h][:, :], Pkk.g(A_), beta_c, dl[h][:, :], ALU.mult, ALU.mult)
                            tt(DVE, attnT[h][:, :], Pqk.g(A_), du[h][:, :], ALU.mult)
                            yield
                            if g == 0 and t == 0 and h == 0:
                                chk("d2")
                            tr(PU.g(A_), Lm[h][:, :], ident_f[:, :])
                            copy(ACT, Um[h][:, :], PU.g(A_))
                            tt(DVE, Ym[h][:, :], ident_f[:, :], Um[h][:, :], ALU.subtract)
                            yield
                            if g == 0 and t == 0 and h == 0:
                                chk("d2b")
                            Pc, Ptc = Um[h], Lm[h]
                            for it in range(5):
                                Pn, Ptn = PA_[2 * h + it % 2], PB_[2 * h + it % 2]
                                s0, s1_, s2_ = NP[0], NP[1], NP[2]
                                if it < 4:
                                    mm(s0.g(A_), Ptc[:, :], Pc[:, :])
                                mm(s1_.g(A_), Pc[:, :], Ptc[:, :])
                                copy(ACT, Ptn[:, :], s1_.g(A_))
                                if it < 4:
                                    copy(ACT, Pn[:, :], s0.g(A_))
                                yield
                                mm(s2_.g(A_), Ptn[:, :], Ym[h][:, :])
                                tt(DVE, Ym[h][:, :], Ym[h][:, :], s2_.g(A_), ALU.add)
                                yield
                                Pc, Ptc = Pn, Ptn
                                if g == 0 and t == 0 and h == 0 and it == 0:
                                    chk("d2c")
                            copy(ACT, Yb[h][:, :], Ym[h][:, :])
                            yield
                            if g == 0 and t == 0 and h == 0:
                                chk("d3")
                            pb0, pb1 = PSB[0], PSB[1]
                            tr(V(PSBt[:, 0:128], pb0.k), kT_t, ident_b[:, :])
                            ts(DVE, kbt[h][:, :], V(PSBt[:, 0:128], pb0.k), bege_c, None, ALU.mult)
                            ts(DVE, kdec[h][:, :], V(PSBt[:, 0:128], pb0.k), ekd_c, None, ALU.mult)
                            yield
                            tr(V(PSBt[:, 512:640], pb1.k), vbT[h][:, cols], ident_b[:, :])
                            ts(DVE, vbt[h][:, :], V(PSBt[:, 512:640], pb1.k), beta_c, None, ALU.mult)
                            yield
                            mm(CP[0].g(A_), Yb[h][:, :], vbt[h][:, :])
                            copy(ACT, wv[h][:, :], CP[0].g(A_))
                            mm(CP[1].g(A_), kbt[h][:, :], Yb[h][:, :])
                            copy(ACT, wkT[h][:, :], CP[1].g(A_))
                            yield
                            if g == 0 and t == 0 and h == 0:
                                chk("d4")
                            for c_ in range(2):
                                pr = slice(c_ * 64, c_ * 64 + 64)
                                Ap = (pr, slice(0, 128))
                                Pu_, Pqs, Pau, PSs = CP[2], CP[3], NP[3], CP[0]
                                mm(Pu_.g(Ap), wkT[h][:, pr], Sb[h][:, :])
                                mm(Pqs.g(Ap), qbT[h][:, t * 128 + c_ * 64:t * 128 + c_ * 64 + 64], Sb[h][:, :])
                                tt(DVE, ub[h][pr, :], wv[h][pr, :], Pu_.g(Ap), ALU.subtract)
                                yield
                                mm(PSs.g(A_), kdec[h][pr, :], ub[h][pr, :])
                                mm(Pau.g(Ap), attnT[h][pr, pr], ub[h][pr, :])
                                act(tq[h][pr, :], Pqs.g(Ap), AF.Identity, scale=V(egc_c.ap[pr, :], egc_c.k))
                                tt(DVE, oraw[h][pr, :], tq[h][pr, :], Pau.g(Ap), ALU.add)
                                yield
                                egl_c = sc[:, 16 + 2 * c_ + h:17 + 2 * c_ + h]
                                stt(Sb[h][:, :], Sf[h][:, :], egl_c, PSs.g(A_), ALU.mult, ALU.add)
                                stt(Sf[h][:, :], Sf[h][:, :], egl_c, PSs.g(A_), ALU.mult, ALU.add)
                                yield
                                if g == 0 and t == 0 and h == 0 and c_ == 0:
                                    chk("d5a")
                                if g == 0 and t == 0 and h == 0 and c_ == 1:
                                    chk("d5")
                            act(on_[h][:, :], oraw[h][:, :], AF.Square, accum=sc[:, 22 + h:23 + h])
                            ts(DVE, sc[:, 24 + h:25 + h], sc[:, 22 + h:23 + h], 1.0 / 128, EPS, ALU.mult, ALU.add)
                            act(sc[:, 26 + h:27 + h], sc[:, 24 + h:25 + h], AF.Ln)
                            act(sc[:, 28 + h:29 + h], sc[:, 26 + h:27 + h], AF.Exp, scale=-0.5)
                            stt(on_[h][:, :], oraw[h][:, :], sc[:, 28 + h:29 + h], onw[:, :], ALU.mult, ALU.mult)
                            yield
                            tt(DVE, obb[h][:, :], on_[h][:, :], siluz[:, t, h * 128:(h + 1) * 128], ALU.mult)
                            tr(V(PSBt[:, 128:256], pb0.k), obb[h][:, :], ident_b[:, :])
                            copy(DVE, stage[2 + h][:, cols], V(PSBt[:, 128:256], pb0.k))
                            if g == 0 and t == 0 and h == 0:
                                chk("d6")
                        gens = [gdn_head(0), gdn_head(1)]
                        if GDN_MODE == 0:
                            for gen_ in gens:
                                for _ in gen_:
                                    pass
                            gens = []
                        while gens:
                            for gen_ in list(gens):
                                try:
                                    next(gen_)
                                except StopIteration:
                                    gens.remove(gen_)
                    for h in range(2):
                        dma(SP, V(sendv[(g * 512) // TOWN, 1, h, :, (g * 512) % TOWN:(g * 512) % TOWN + 512], send.k),
                            stage[2 + h][:, :], nowaw=True)
                    if g == 0:
                        dump(stage[2][:, :], 5120, 512)
                        dump(stage[3][:, :], 5632, 512)
                    K.barrier()
                    chk("1d")
            K.barrier()

        POOL.issue(lambda: nc.gpsimd.collective_compute(
            "AllGather", ALU.bypass, replica_groups=[list(range(NCORE))], ins=[send.t], outs=[recv.t]),
            [send.k], [recv.k])
        POOL._wait(POOL.sid, POOL.cnt)
        chk("ag")
        rank = nc.gpsimd.partition_id()
        rv = recv.t.rearrange("(s t a h p) c -> t a p s h c", s=8, t=8, a=2, h=2)

        with ExitStack() as e2:
            g1bc = K.sb([128, D], F32, e2)
            g2bc = K.sb([128, D], F32, e2)
            with ExitStack() as eg:
                dg = [K.sb([128, 128], F32, eg) for _ in range(2)]
                for which, dst in ((32, g1bc), (80, g2bc)):
                    for j in range(NK):
                        d_ = dg[j % 2]
                        ts(DVE, d_[:, :], ident_f[:, :], modc[:, which + j:which + j + 1], None, ALU.mult)
                        P = PS[j % 2]
                        mm(P[:, 0:128], ones_f[:, :], d_[:, :])
                        copy(ACT, dst[:, j * 128:(j + 1) * 128], P[:, 0:128])
                K.barrier()
            for half in range(NHALF):
                hc = slice(half * 512, half * 512 + 512)
                with ExitStack() as eh:
                    h1 = [K.sb([128, D], F32, eh) for _ in range(4)]
                    mT = K.sb([128, NK, 512], BF16, eh)
                    with ExitStack() as eb:
                        oaT = K.sb([128, NK, 512], BF16, eb)
                        obT = K.sb([128, NK, 512], BF16, eb)
                        uTo = K.sb([128, NK, 512], BF16, eb)
                        for ab, dst in ((0, oaT), (1, obT)):
                            src = rv[bass.ds(rank, 1), ab].rearrange("o p s h c -> (o p) s h c")
                            for hh in range(2):
                                POOL.issue(lambda _d=dst, _s=src, _h=hh: nc.gpsimd.dma_start(
                                    out=_d.t[:, :, :].rearrange("p (s h) c -> p s h c", s=8)[:, :, _h, :],
                                    in_=_s[:, :, _h, hc]),
                                    [recv.k], [dst.k], dma_k=dst.k, nowaw=True)
                        with ExitStack() as ea:
                            def load_xo(t, dst, _half=half):
                                r0 = _half * 512 + t * 128
                                dma(SP, dst[:, :], x_own[r0:r0 + 128, :])
                                return dst
                            make_uT(ea, load_xo, 4, a1, b1, uTo)
                            K.barrier()
                        for t in range(4):
                            r0 = half * 512 + t * 128
                            dma(SP, h1[t][:, :], x_own[r0:r0 + 128, :])
                        with ExitStack() as ew:
                            slab = [[K.sb([128, NK, 256], BF16, ew) for _ in range(4)] for _ in range(2)]
                            Fm = [K.sb([128, 512], F32, ew) for _ in range(4)]
                            srcs = [(p_a, 0), (p_b, 0), (w_g, 0), (w_g, D)]
                            for c2 in range(8):
                                sl_ = slab[c2 % 2]
                                for wi, (wt, off) in enumerate(srcs):
                                    wvv = wt.t.rearrange("(k p) n -> p k n", p=128)
                                    dma(POOL, sl_[wi][:, :, :], V(wvv[:, :, off + c2 * 256:off + (c2 + 1) * 256], wt.k))
                                for cc_ in range(2):
                                    jc = c2 * 2 + cc_
                                    wc = slice(cc_ * 128, cc_ * 128 + 128)
                                    Ps = [PS[(jc % 2) * 3 + i] if i < 3 else PS[6] for i in range(4)]
                                    acts = [oaT, obT, uTo, uTo]
                                    for wi in range(4):
                                        for k in range(NK):
                                            mm(Ps[wi][:, :], sl_[wi][:, k, wc], acts[wi][:, k, :],
                                               start=(k == 0), stop=(k == NK - 1), defer=(k != NK - 1))
                                    act(Fm[0][:, :], Ps[2][:, :], AF.Sigmoid)
                                    act(Fm[1][:, :], Ps[3][:, :], AF.Sigmoid)
                                    tt(DVE, Fm[2][:, :], Fm[0][:, :], Ps[0][:, :], ALU.mult)
                                    tt(DVE, Fm[3][:, :], Fm[1][:, :], Ps[1][:, :], ALU.mult)
                                    tt(DVE, mT[:, jc, :], Fm[2][:, :], Fm[3][:, :], ALU.add)
                            K.barrier()
                    with ExitStack() as ew:
                        slab = [K.sb([128, NK, 512], BF16, ew) for _ in range(2)]
                        Fm = [K.sb([128, 512], F32, ew) for _ in range(2)]
                        wvv = w_out.t.rearrange("(k p) n -> p k n", p=128)
                        for cg in range(4):
                            sl_ = slab[cg % 2]
                            cs_ = slice(cg * 512, cg * 512 + 512)
                            dma(POOL, sl_[:, :, :], V(wvv[:, :, cs_], w_out.k))
                            for t in range(4):
                                P = PS[(cg * 4 + t) % 4]
                                for k in range(NK):
                                    mm(P[:, :], mT[:, k, t * 128:(t + 1) * 128], sl_[:, k, :],
                                       start=(k == 0), stop=(k == NK - 1), defer=(k != NK - 1))
                                f_ = Fm[t % 2]
                                tt(DVE, f_[:, :], P[:, :], g1bc[:, cs_], ALU.mult)
                                tt(DVE, h1[t][:, cs_], f_[:, :], h1[t][:, cs_], ALU.add)
                        K.barrier()
                    with ExitStack() as ef:
                        u2T = K.sb([128, NK, 512], BF16, ef)
                        ffT = K.sb([128, NF, 512], BF16, ef)
                        with ExitStack() as ea:
                            def load_h(t, dst):
                                return h1[t]
                            make_uT(ea, load_h, 4, a2, b2, u2T)
                            K.barrier()
                        with ExitStack() as ew:
                            slab = [[K.sb([128, NK, 256], BF16, ew) for _ in range(2)] for _ in range(2)]
                            Fm = [K.sb([128, 512], F32, ew) for _ in range(2)]
                            wgv = w_gate.t.rearrange("(k p) n -> p k n", p=128)
                            wuv = w_up.t.rearrange("(k p) n -> p k n", p=128)
                            for f2 in range(NF // 2):
                                sl_ = slab[f2 % 2]
                                cs_ = slice(f2 * 256, f2 * 256 + 256)
                                dma(POOL, sl_[0][:, :, :], V(wgv[:, :, cs_], w_gate.k))
                                dma(POOL, sl_[1][:, :, :], V(wuv[:, :, cs_], w_up.k))
                                for cc_ in range(2):
                                    f = f2 * 2 + cc_
                                    wc = slice(cc_ * 128, cc_ * 128 + 128)
                                    Pg_, Pu2 = PS[(f % 2) * 2], PS[(f % 2) * 2 + 1]
                                    for wi, P in ((0, Pg_), (1, Pu2)):
                                        for k in range(NK):
                                            mm(P[:, :], sl_[wi][:, k, wc], u2T[:, k, :],
                                               start=(k == 0), stop=(k == NK - 1), defer=(k != NK - 1))
                                    f_ = Fm[f % 2]
                                    act(f_[:, :], Pg_[:, :], AF.Silu)
                                    tt(DVE, ffT[:, f, :], f_[:, :], Pu2[:, :], ALU.mult)
                            K.barrier()
                        with ExitStack() as ew:
                            slab = [K.sb([128, NF, 256], BF16, ew) for _ in range(2)]
                            Fm = [K.sb([128, 256], F32, ew) for _ in range(2)]
                            wdv = w_down.t.rearrange("(f p) n -> p f n", p=128)
                            for cg in range(8):
                                sl_ = slab[cg % 2]
                                cs_ = slice(cg * 256, cg * 256 + 256)
                                dma(POOL, sl_[:, :, :], V(wdv[:, :, cs_], w_down.k))
                                for t in range(4):
                                    P = PS[(cg * 4 + t) % 4]
                                    for f in range(NF):
                                        mm(P[:, 0:256], ffT[:, f, t * 128:(t + 1) * 128], sl_[:, f, :],
                                           start=(f == 0), stop=(f == NF - 1), defer=(f != NF - 1))
                                    f_ = Fm[t % 2]
                                    tt(DVE, f_[:, :], P[:, 0:256], g2bc[:, cs_], ALU.mult)
                                    tt(DVE, h1[t][:, cs_], f_[:, :], h1[t][:, cs_], ALU.add)
                            K.barrier()
                    for t in range(4):
                        r0 = half * 512 + t * 128
                        dma(SP, out_d[r0:r0 + 128, :], h1[t][:, :], nowaw=True)
                    K.barrier()
                    for sid, val in out_d.k.w.items():
                        SP._wait(sid, val)
    return nc


def _consts():
    i = np.arange(128)
    same = (i[:, None] // 64) == (i[None, :] // 64)
    ident = np.eye(128, dtype=np.float32)
    tri2 = ((i[:, None] <= i[None, :]) & same).astype(np.float32)
    bones = same.astype(np.float32)
    mLs = ((i[:, None] > i[None, :]) & same).astype(np.float32)
    mUi = ((i[None, :] >= i[:, None]) & same).astype(np.float32)
    negU = -(i[:, None] >= i[None, :]).astype(np.float32)
    tq = np.arange(512)
    md = [((r * 128 + i[:, None]) < tq[None, :]).astype(np.float32) for r in range(4)]
    sel = np.stack([(i // 64) == 0, (i // 64) == 1], axis=1).astype(np.float32)
    return np.ascontiguousarray(np.concatenate([ident, tri2, bones, mLs, mUi, negU] + md + [sel], axis=1))


_NC_CACHE = {}


def kernel(x, c, w_mod, b_mod, norm1_w, w_in, q_norm_w, k_norm_w, conv_w, a_log, dt_bias,
           o_norm_w, p_a, p_b, w_out, norm2_w, w_gate, w_up, w_down):
    f = lambda a: np.ascontiguousarray(np.asarray(a, dtype=np.float32))
    x = f(x)
    SEQ = x.shape[1]
    TOWN = SEQ // NCORE
    xa = x[0]
    col16 = lambda v: f(np.asarray(v).reshape(NK, 128).T)
    w_in0 = f(w_in)[0]
    conv0 = f(conv_w)[0]
    wm0 = f(w_mod)[0]
    bm0 = f(np.asarray(b_mod)[0].reshape(96, 128).T)
    shared = {
        "x_all": xa,
        "c_col": col16(np.asarray(c)[0]),

        "n1w_col": col16(np.asarray(norm1_w)[0]),
        "n2w_col": col16(np.asarray(norm2_w)[0]),
        "w_g": f(w_in0[:, 14368:18464]),
        "qnw_col": f(np.asarray(q_norm_w)[0].reshape(128, 1)),
        "knw_col": f(np.asarray(k_norm_w)[0].reshape(128, 1)),
        "onw_bc": f(np.broadcast_to(np.asarray(o_norm_w)[0][None, :], (128, 128))),
        "p_a": f(p_a)[0], "p_b": f(p_b)[0], "w_out": f(w_out)[0],
        "w_gate": f(w_gate)[0], "w_up": f(w_up)[0], "w_down": f(w_down)[0],
        "cst": _consts(),
    }
    if STOP is not None:
        for k_ in ("w_g", "p_a", "p_b", "w_out", "w_gate", "w_up", "w_down"):
            shared.pop(k_)
    in_maps = []
    for core in range(NCORE):
        hs = [2 * core, 2 * core + 1]
        cols = []
        for base in (0, 2048):
            for h in hs:
                cols.append(np.arange(base + h * 128, base + (h + 1) * 128))
        for base in (6144, 6144 + 2048, 6144 + 4096):
            for h in hs:
                cols.append(np.arange(base + h * 128, base + (h + 1) * 128))
        for base in (4096, 12288):
            for h in hs:
                cols.append(np.arange(base + h * 128, base + (h + 1) * 128))
        cols.append(np.array([14336 + hs[0], 14336 + hs[1], 14352 + hs[0], 14352 + hs[1]]))
        cols = np.concatenate(cols)
        cw = np.zeros((128, 24), np.float32)
        for hi, h in enumerate(hs):
            for typ in range(3):
                blk = conv0[:, typ * 2048 + h * 128: typ * 2048 + (h + 1) * 128]
                cw[:, (hi * 3 + typ) * 4:(hi * 3 + typ) * 4 + 4] = blk.T
        m = dict(shared)
        m["x_own"] = np.ascontiguousarray(xa[core * TOWN:(core + 1) * TOWN])
        m["w_mod"] = np.ascontiguousarray(wm0[:, core * 1536:(core + 1) * 1536])
        m["bmod_col"] = np.ascontiguousarray(bm0[:, core * 12:(core + 1) * 12])
        m["w_h"] = np.ascontiguousarray(w_in0[:, cols])
        m["convw"] = cw
        m["alog_bc"] = f(np.broadcast_to(np.asarray(a_log)[0][hs][None, :], (128, 2)))
        m["dtb_bc"] = f(np.broadcast_to(np.asarray(dt_bias)[0][hs][None, :], (128, 2)))
        in_maps.append(m)
    if SEQ not in _NC_CACHE:
        _NC_CACHE[SEQ] = build(SEQ)
    nc = _NC_CACHE[SEQ]
    res = run_bass_kernel_spmd(nc, in_maps, core_ids=list(range(NCORE)))
    out = np.concatenate([np.asarray(r["out"]) for r in res.results], axis=0)
    return out.reshape(1, SEQ, D).astype(np.float32)
```

```python
import numpy as np
from contextlib import ExitStack
import concourse.bass as bass
import concourse.mybir as mybir
from concourse.bass_utils import run_bass_kernel_spmd

F32 = mybir.dt.float32
BF16 = mybir.dt.bfloat16
AF = mybir.ActivationFunctionType
ALU = mybir.AluOpType

D = 2048
NK = 16
DFF = 5632
NF = 44
EPS = 1e-6
NCORE = 8
WH = 1796


_RECORD = False
_NEEDED = {}
_RANK = {}


class Trk:
    __slots__ = ("w", "r", "dsem", "dval", "dq")

    def __init__(s):
        s.w = {}
        s.r = {}
        s.dsem = None
        s.dval = 0
        s.dq = None


class V:
    __slots__ = ("ap", "k")

    def __init__(s, ap, k):
        s.ap = ap
        s.k = k


class TT:
    def __init__(s, t, k=None):
        s.t = t
        s.k = k if k is not None else Trk()

    def __getitem__(s, idx):
        return V(s.t[idx], s.k)


class Eng:
    def __init__(s, K, name, eng, same):
        s.K = K
        s.name = name
        s.eng = eng
        s.same = same
        s.own = set()
        s.sid = K.newsem(name)
        K.tl[s.sid] = name
        s.own.add(s.sid)
        s.cnt = 0
        s.last = None
        s.seen = {}
        s.pend_r = []
        s.pend_w = []

    def _wait(s, sid, val):
        if s.seen.get(sid, 0) >= val:
            return
        s.seen[sid] = val
        nm = s.K.tl.get(sid)
        if nm is not None:
            if _RECORD:
                _NEEDED.setdefault(nm, set()).add(val)
            else:
                val = _RANK[nm][val]
        s.eng.wait_ge(s.K.h[sid], val)

    def issue(s, fn, r, w, dma_k=None, nowaw=False, defer=False):
        need = {}
        for t in r:
            for sid, val in t.w.items():
                if sid in s.own and not s.same:
                    continue
                if need.get(sid, 0) < val:
                    need[sid] = val
        for t in w:
            if not nowaw:
                for sid, val in t.w.items():
                    if sid in s.own and not s.same:
                        continue
                    if need.get(sid, 0) < val:
                        need[sid] = val
            for sid, val in t.r.items():
                if sid in s.own and not s.same:
                    continue
                if need.get(sid, 0) < val:
                    need[sid] = val
        for sid, val in need.items():
            s._wait(sid, val)
        ins = fn()
        if dma_k is not None:
            k = dma_k
            s.K.all_dma.add(k)
            if k.dsem is None:
                fl = s.K.free_dsems.setdefault(s.name, [])
                k.dq = s.name
                if fl:
                    k.dsem, k.dval = fl.pop()
                else:
                    k.dsem = s.K.newsem("d")
            assert k.dq == s.name, (k.dq, s.name)
            k.dval += 16
            ins.then_inc(s.K.h[k.dsem], 16)
            tok = (k.dsem, k.dval)
        else:
            if defer:
                s.pend_r += r
                s.pend_w += w
                return
            s.cnt += 1
            if _RECORD or s.cnt in _RANK.get(s.name, {}):
                ins.then_inc(s.K.h[s.sid], 1)
            tok = (s.sid, s.cnt)
            s.last = tok
            r = r + s.pend_r
            w = w + s.pend_w
            s.pend_r = []
            s.pend_w = []
        for t in r:
            if t.r.get(tok[0], 0) < tok[1]:
                t.r[tok[0]] = tok[1]
        for t in w:
            if nowaw:
                t.w[tok[0]] = tok[1]
            else:
                t.w = {tok[0]: tok[1]}
            t.r = {}


class Kern:
    def __init__(s, nc, es):
        s.nc = nc
        s.es = es
        s.h = []
        s.uid = 0
        s.free_dsems = {}
        s.all_dma = set()
        s.tl = {}
        s.PE = Eng(s, "pe", nc.tensor, False)
        s.ACT = Eng(s, "act", nc.scalar, True)
        s.DVE = Eng(s, "dve", nc.vector, True)
        s.POOL = Eng(s, "pool", nc.gpsimd, True)
        s.SP = Eng(s, "sp", nc.sync, True)
        s.engs = [s.PE, s.ACT, s.DVE, s.POOL, s.SP]

    def newsem(s, name):
        s.uid += 1
        sem = s.es.enter_context(s.nc.semaphore(f"{name}{s.uid}"))
        s.h.append(sem)
        return len(s.h) - 1

    def sb(s, shape, dt, es=None, name="t"):
        s.uid += 1
        t = (es or s.es).enter_context(s.nc.sbuf_tensor(f"{name}{s.uid}", list(shape), dt))
        o = TT(t)
        if es is not None:
            def _rel(k=o.k, K=s):
                if k.dsem is not None:
                    K.free_dsems.setdefault(k.dq, []).append((k.dsem, k.dval))
                    k.dsem = None
            es.callback(_rel)
        return o

    def barrier(s):
        for e in s.engs:
            for o in s.engs:
                if o is not e and o.last is not None:
                    e._wait(o.last[0], o.last[1])

    def do(s, E, meth, outs, ins, nowaw=False, defer=False, **kw):
        r = [v.k for v in ins.values() if isinstance(v, V)]
        w = [v.k for v in outs.values()]
        args = {}
        for k_, v in list(outs.items()) + list(ins.items()):
            args[k_] = v.ap if isinstance(v, V) else v
        args.update(kw)
        E.issue(lambda: getattr(E.eng, meth)(**args), r, w, nowaw=nowaw, defer=defer)

    def mm(s, out, lhsT, rhs, start=True, stop=True, defer=False):
        s.do(s.PE, "matmul", dict(out=out), dict(lhsT=lhsT, rhs=rhs), start=start, stop=stop, defer=defer)

    def tr(s, out, in_, ident, defer=False):
        s.do(s.PE, "transpose", dict(out=out), dict(in_=in_, identity=ident), defer=defer)

    def act(s, out, in_, func, scale=1.0, bias=0.0, accum=None):
        outs = dict(out=out)
        if accum is not None:
            outs["accum_out"] = accum
        s.do(s.ACT, "activation", outs, dict(in_=in_, bias=bias, scale=scale), func=func)

    def ts(s, E, out, in0, s1, s2, op0, op1=None):
        kw = dict(op0=op0)
        if op1 is not None:
            kw["op1"] = op1
        s.do(E, "tensor_scalar", dict(out=out), dict(in0=in0, scalar1=s1, scalar2=s2), **kw)

    def stt(s, out, in0, scalar, in1, op0, op1):
        s.do(s.DVE, "scalar_tensor_tensor", dict(out=out), dict(in0=in0, scalar=scalar, in1=in1), op0=op0, op1=op1)

    def tt(s, E, out, in0, in1, op):
        s.do(E, "tensor_tensor", dict(out=out), dict(in0=in0, in1=in1), op=op)

    def copy(s, E, out, in_):
        if E is s.ACT:
            s.act(out, in_, AF.Identity)
        else:
            s.do(E, "tensor_copy", dict(out=out), dict(in_=in_))

    def memset(s, E, out, val):
        s.do(E, "memset", dict(ap=out), {}, constant=val)

    def dma(s, E, out, in_, nowaw=False):
        r = [in_.k]
        w = [out.k]
        E.issue(lambda: E.eng.dma_start(out=out.ap, in_=in_.ap), r, w, dma_k=out.k, nowaw=nowaw)


class _Stop(Exception):
    pass


import os
STOP = os.environ.get("KSTOP") or None
SIMDBG = False
ATT_SKEW = 1
GDN_MODE = 1


def build(SEQ=8192):
    global _RECORD, _NEEDED, _RANK
    _RECORD, _NEEDED, _RANK = True, {}, {}
    try:
        _build(SEQ)
    except _Stop:
        pass
    _RANK = {nm: {v: i + 1 for i, v in enumerate(sorted(vs))} for nm, vs in _NEEDED.items()}
    _RECORD = False
    try:
        return _build(SEQ)
    except _Stop as e:
        return e.args[0]


def _build(SEQ=8192):
    NT = SEQ // 128
    NG = SEQ // 512
    TOWN = SEQ // NCORE
    NHALF = TOWN // 512
    nc = bass.Bass("TRN2", target_bir_lowering=False)

    def din(name, shape):
        return TT(nc.dram_tensor(name, list(shape), F32, kind="ExternalInput").ap())

    x_all = din("x_all", [SEQ, D])
    x_own = din("x_own", [TOWN, D])
    c_col = din("c_col", [128, NK])
    w_mod = din("w_mod", [D, 1536])
    bmod_col = din("bmod_col", [128, 12])
    msend = TT(nc.dram_tensor("msend", [128, 64], F32).ap())
    mrecv = TT(nc.dram_tensor("mrecv", [8 * 128, 64], F32).ap())
    n1w = din("n1w_col", [128, NK])
    n2w = din("n2w_col", [128, NK])
    w_h = din("w_h", [D, WH])
    qnw = din("qnw_col", [128, 1])
    knw = din("knw_col", [128, 1])
    convw = din("convw", [128, 24])
    alog_bc = din("alog_bc", [128, 2])
    dtb_bc = din("dtb_bc", [128, 2])
    onw_bc = din("onw_bc", [128, 128])
    if STOP is None:
        w_g = din("w_g", [D, 2 * D])
        p_a = din("p_a", [D, D])
        p_b = din("p_b", [D, D])
        w_out = din("w_out", [D, D])
        w_gate = din("w_gate", [D, DFF])
        w_up = din("w_up", [D, DFF])
        w_down = din("w_down", [DFF, D])
    NCST = 6 * 128 + 4 * 512 + 2
    cst = din("cst", [128, NCST])
    out_d = TT(nc.dram_tensor("out", [TOWN, D], F32, kind="ExternalOutput").ap())
    if SIMDBG:
        modc_dbg = din("modc_dbg", [128, 96])
        dbg = TT(nc.dram_tensor("dbg", [128, 8192], F32, kind="ExternalOutput").ap())
    send = TT(nc.dram_tensor("sendb", [8 * 512, TOWN], BF16).ap())
    recv = TT(nc.dram_tensor("recvb", [8 * 8 * 512, TOWN], BF16).ap())

    with ExitStack() as es:
        K = Kern(nc, es)
        PE, ACT, DVE, POOL, SP = K.PE, K.ACT, K.DVE, K.POOL, K.SP

        def dump(v, c0, n):
            if SIMDBG:
                POOL.issue(lambda: nc.gpsimd.dma_start(out=dbg.t[:, c0:c0 + n], in_=v.ap), [v.k], [dbg.k],
                           dma_k=dbg.k, nowaw=True)

        def chk(name):
            if STOP == name:
                K.barrier()
                for k_ in K.all_dma:
                    if k_.dsem is not None:
                        (POOL if k_.dq == "pool" else SP)._wait(k_.dsem, k_.dval)
                for q_, fl_ in K.free_dsems.items():
                    for sid_, val_ in fl_:
                        (POOL if q_ == "pool" else SP)._wait(sid_, val_)
                print("STOP at", name, "sems", len(K.h), [(e.name, e.cnt) for e in K.engs])
                raise _Stop(nc)
        mm, tr, act, ts, stt, tt, copy, memset, dma = K.mm, K.tr, K.act, K.ts, K.stt, K.tt, K.copy, K.memset, K.dma

        PSt = [es.enter_context(nc.psum_tensor(f"ps{i}", [128, 512], F32)) for i in range(7)]
        PS = [TT(t) for t in PSt]
        PSBt = es.enter_context(nc.psum_tensor("psb", [128, 1024], BF16))
        _pk = Trk()
        PSB = [TT(PSBt, _pk), TT(PSBt, _pk)]

        def pslot(i, c0, c1, p0=0, p1=128):
            class _S:
                pass
            o = _S()
            o.k = PS[i].k

            def gi(idx, _i=i, _c0=c0):
                ps, cs = idx
                a = _c0 + (cs.start or 0)
                b = _c0 + (cs.stop if cs.stop is not None else (c1 - c0))
                return V(PSt[_i][ps, a:b], o.k)
            o.g = gi
            return o

        ident_f = K.sb([128, 128], F32)
        ident_b = K.sb([128, 128], BF16)
        ones_f = K.sb([128, 128], F32)
        nones_f = K.sb([128, 128], F32)
        ones_b = K.sb([128, 128], BF16)
        nones_b = K.sb([128, 128], BF16)
        tri2 = K.sb([128, 128], F32)
        bones = K.sb([128, 128], F32)
        mLs = K.sb([128, 128], F32)
        mUi = K.sb([128, 128], F32)
        negU = K.sb([128, 128], BF16)
        mdiag = K.sb([128, 4, 512], BF16)
        sel = K.sb([128, 2], F32)
        with ExitStack() as e0:
            cs_t = K.sb([128, NCST], F32, e0)
            dma(SP, cs_t[:, :], cst[:, :])
            copy(DVE, ident_f[:, :], cs_t[:, 0:128])
            copy(DVE, ident_b[:, :], cs_t[:, 0:128])
            copy(DVE, tri2[:, :], cs_t[:, 128:256])
            copy(DVE, bones[:, :], cs_t[:, 256:384])
            copy(DVE, mLs[:, :], cs_t[:, 384:512])
            copy(DVE, mUi[:, :], cs_t[:, 512:640])
            copy(DVE, negU[:, :], cs_t[:, 640:768])
            for r_ in range(4):
                copy(DVE, mdiag[:, r_, :], cs_t[:, 768 + 512 * r_:768 + 512 * (r_ + 1)])
            copy(DVE, sel[:, :], cs_t[:, 768 + 2048:768 + 2050])
            memset(DVE, ones_f[:, :], 1.0)
            memset(DVE, nones_f[:, :], -1.0)
            memset(DVE, ones_b[:, :], 1.0)
            memset(DVE, nones_b[:, :], -1.0)
            K.barrier()

        modc = K.sb([128, 96], F32)
        a1 = K.sb([128, NK], F32)
        a2 = K.sb([128, NK], F32)
        small = K.sb([128, 64], F32)
        with ExitStack() as e0:
            cc = K.sb([128, NK], F32, e0)
            cb = K.sb([128, NK], BF16, e0)
            bm = K.sb([128, 12], F32, e0)
            modl = K.sb([128, 64], F32, e0)
            memset(DVE, modl[:, :], 0.0)
            nw1 = K.sb([128, NK], F32, e0)
            nw2 = K.sb([128, NK], F32, e0)
            slabs = [K.sb([128, NK, 512], BF16, e0) for _ in range(3)]
            dma(SP, cc[:, :], c_col[:, :])
            if not SIMDBG:
                dma(SP, bm[:, :], bmod_col[:, :])
            dma(SP, nw1[:, :], n1w[:, :])
            dma(SP, nw2[:, :], n2w[:, :])
            act(cb[:, :], cc[:, :], AF.Silu)
            wm = w_mod.t.rearrange("(k p) n -> p k n", p=128)
            pm = PS[0]
            for sl in range(0 if SIMDBG else 3):
                S_ = slabs[sl % 3]
                dma(POOL, S_[:, :, :], V(wm[:, :, sl * 512:(sl + 1) * 512], w_mod.k))
                for jb in range(4):
                    j = sl * 4 + jb
                    for k in range(NK):
                        mm(pm[:, j:j + 1], S_[:, k, jb * 128:(jb + 1) * 128], cb[:, k:k + 1],
                           start=(k == 0), stop=(k == NK - 1), defer=(k != NK - 1))
            if SIMDBG:
                dma(SP, modc[:, :], modc_dbg[:, :])
            else:
              tt(DVE, modl[:, 0:12], pm[:, 0:12], bm[:, :], ALU.add)
              dma(SP, msend[:, :], modl[:, :])
              POOL.issue(lambda: nc.gpsimd.collective_compute(
                "AllGather", ALU.bypass, replica_groups=[list(range(NCORE))], ins=[msend.t], outs=[mrecv.t]),
                [msend.k], [mrecv.k])
              POOL._wait(POOL.sid, POOL.cnt)
              dma(SP, V(modc.t[:, :].rearrange("p (r j) -> p r j", r=8), modc.k),
                  V(mrecv.t.rearrange("(r p) j -> p r j", p=128)[:, :, 0:12], mrecv.k))
            stt(a1[:, :], modc[:, 16:32], 1.0, nw1[:, :], ALU.add, ALU.mult)
            stt(a2[:, :], modc[:, 64:80], 1.0, nw2[:, :], ALU.add, ALU.mult)
            K.barrier()
        chk("p0")
        b1 = lambda k: modc[:, k:k + 1]
        b2 = lambda k: modc[:, 48 + k:49 + k]

        def make_uT(es_, load_tile, n_tiles, acol, bcol, uT):
            xts = [K.sb([128, D], F32, es_) for _ in range(2)]
            xns = [K.sb([128, D], BF16, es_) for _ in range(4)]
            st = K.sb([128, 8], F32, es_)
            for t in range(n_tiles):
                xt = load_tile(t, xts[t % 2])
                xn = xns[t % 4]
                act(xn[:, :], xt[:, :], AF.Square, accum=st[:, 0:1])
                ts(DVE, st[:, 1:2], st[:, 0:1], 1.0 / D, EPS, ALU.mult, ALU.add)
                act(st[:, 2:3], st[:, 1:2], AF.Ln)
                act(st[:, 3:4], st[:, 2:3], AF.Exp, scale=-0.5)
                ts(DVE, xn[:, :], xt[:, :], st[:, 3:4], None, ALU.mult)
            for k in range(NK):
                pb = PSB[k % 2]
                c0 = (k % 2) * 512
                for t in range(n_tiles):
                    tr(V(PSBt[:, c0 + t * 128:c0 + (t + 1) * 128], pb.k), xns[t % 4][:, k * 128:(k + 1) * 128],
                       ident_b[:, :], defer=(t != n_tiles - 1))
                src = V(PSBt[:, c0:c0 + n_tiles * 128], pb.k)
                if k % 2 == 0:
                    ts(DVE, uT[:, k, :], src, acol[:, k:k + 1], bcol(k), ALU.mult, ALU.add)
                else:
                    act(uT[:, k, :], src, AF.Identity, scale=acol[:, k:k + 1], bias=bcol(k))

        with ExitStack() as e1:
            Wh = K.sb([128, NK, WH], BF16, e1)
            whv = w_h.t.rearrange("(k p) n -> p k n", p=128)
            for k in range(NK):
                dma(POOL, Wh[:, k, :], V(whv[:, k, :], w_h.k), nowaw=True)
            KT = [K.sb([128, SEQ], BF16, e1) for _ in range(2)]
            VA = [K.sb([128, NT, 128], BF16, e1) for _ in range(2)]
            uT = K.sb([128, NK, 512], BF16, e1)
            qT = [K.sb([128, 512], BF16, e1) for _ in range(2)]
            qbT = [K.sb([128, 512], BF16, e1) for _ in range(2)]
            kbT = [K.sb([128, 512], BF16, e1) for _ in range(2)]
            vbT = [K.sb([128, 512], BF16, e1) for _ in range(2)]
            hist = [K.sb([128, 4], F32, e1) for _ in range(6)]
            siluz = K.sb([128, 4, 256], BF16, e1)
            betag = K.sb([128, 4, 4], F32, e1)
            stage = [K.sb([128, 512], BF16, e1) for _ in range(4)]
            Sf = [K.sb([128, 128], F32, e1) for _ in range(2)]
            Sb = [K.sb([128, 128], BF16, e1) for _ in range(2)]
            cw = K.sb([128, 24], F32, e1)
            qcol = K.sb([128, 4], F32, e1)
            negea = K.sb([128, 2], F32, e1)
            dtb = K.sb([128, 2], F32, e1)
            onw = K.sb([128, 128], F32, e1)
            dma(SP, cw[:, :], convw[:, :])
            dma(SP, qcol[:, 2:3], qnw[:, :])
            dma(SP, qcol[:, 1:2], knw[:, :])
            dma(SP, negea[:, :], alog_bc[:, :])
            dma(SP, dtb[:, :], dtb_bc[:, :])
            dma(SP, onw[:, :], onw_bc[:, :])
            ts(DVE, qcol[:, 0:1], qcol[:, 2:3], float(128 ** -0.5), None, ALU.mult)
            act(negea[:, :], negea[:, :], AF.Exp)
            ts(DVE, negea[:, :], negea[:, :], -1.0, None, ALU.mult)
            for h_ in hist:
                memset(DVE, h_[:, :], 0.0)
            for h in range(2):
                memset(DVE, Sf[h][:, :], 0.0)
                memset(DVE, Sb[h][:, :], 0.0)
            K.barrier()
            sendv = send.t.rearrange("(t a h p) c -> t a h p c", t=8, a=2, h=2)

            for g in range(NG):
                with ExitStack() as ea:
                    def load_x(t, dst, _g=g):
                        ti = _g * 4 + t
                        dma(SP, dst[:, :], x_all[ti * 128:(ti + 1) * 128, :])
                        return dst
                    make_uT(ea, load_x, 4, a1, b1, uT)
                    if g == 0:
                        dump(uT[:, 0, :], 0, 512)
                        dump(uT[:, 5, :], 512, 512)
                    K.barrier()
                    chk("1a")
                with ExitStack() as eb:
                    F = [K.sb([128, 512], F32, eb) for _ in range(4)]
                    raw = [K.sb([128, 516], F32, eb) for _ in range(2)]
                    sqb = [K.sb([128, 512], BF16, eb) for _ in range(2)]
                    sm = K.sb([128, 16], F32, eb)
                    for blk in range(10):
                        P = PS[blk % 4]
                        for k in range(NK):
                            mm(P[:, :], Wh[:, k, blk * 128:(blk + 1) * 128], uT[:, k, :],
                               start=(k == 0), stop=(k == NK - 1), defer=(k != NK - 1))
                        if blk < 4:
                            h = blk % 2
                            isq = blk < 2
                            sq = sqb[blk % 2]
                            act(sq[:, :], P[:, :], AF.Square)
                            Pss = PS[4 + blk % 2]
                            mm(Pss[:, :], ones_b[:, :], sq[:, :])
                            lnv = F[0]
                            rs = F[1]
                            ts(DVE, lnv[:, :], Pss[:, :], 1.0 / 128, EPS, ALU.mult, ALU.add)
                            act(lnv[:, :], lnv[:, :], AF.Ln)
                            act(rs[:, :], lnv[:, :], AF.Exp, scale=-0.5)
                            dst = qT[h][:, :] if isq else KT[h][:, g * 512:(g + 1) * 512]
                            stt(dst, P[:, :], qcol[:, 0:1] if isq else qcol[:, 1:2], rs[:, :], ALU.mult, ALU.mult)
                        else:
                            s_ = blk - 4
                            typ = s_ // 2
                            h = s_ % 2
                            rw = raw[s_ % 2]
                            copy(ACT, rw[:, 3:515], P[:, :])
                            copy(DVE, rw[:, 0:3], hist[s_][:, 0:3])
                            acc = F[2]
                            ci = (h * 3 + typ) * 4
                            ts(DVE, acc[:, :], rw[:, 0:512], cw[:, ci:ci + 1], None, ALU.mult)
                            for j in range(1, 4):
                                stt(acc[:, :], rw[:, j:j + 512], cw[:, ci + j:ci + j + 1], acc[:, :], ALU.mult, ALU.add)
                            copy(DVE, hist[s_][:, 0:3], rw[:, 512:515])
                            if typ == 2:
                                act(vbT[h][:, :], acc[:, :], AF.Silu)
                            else:
                                slt = F[3]
                                act(slt[:, :], acc[:, :], AF.Silu)
                                sq = sqb[blk % 2]
                                act(sq[:, :], slt[:, :], AF.Square)
                                Pss = PS[4 + blk % 2]
                                mm(Pss[:, :], ones_b[:, :], sq[:, :])
                                lnv = F[0]
                                rs = F[1]
                                ts(DVE, lnv[:, :], Pss[:, :], EPS, None, ALU.add)
                                act(lnv[:, :], lnv[:, :], AF.Ln)
                                act(rs[:, :], lnv[:, :], AF.Exp, scale=-0.5)
                                dst = qbT[h] if typ == 0 else kbT[h]
                                stt(dst[:, :], slt[:, :], float(128 ** -0.5) if typ == 0 else 1.0, rs[:, :],
                                    ALU.mult, ALU.mult)
                    for t in range(4):
                        ti = g * 4 + t
                        P1 = PS[t % 2]
                        P2 = PS[2 + t % 2]
                        for k in range(NK):
                            mm(P1[:, :], uT[:, k, t * 128:(t + 1) * 128], Wh[:, k, 1280:1792],
                               start=(k == 0), stop=(k == NK - 1), defer=True)
                        for k in range(NK):
                            mm(P2[:, 0:4], uT[:, k, t * 128:(t + 1) * 128], Wh[:, k, 1792:1796],
                               start=(k == 0), stop=(k == NK - 1), defer=(k != NK - 1))
                        copy(ACT, VA[0][:, ti, :], P1[:, 0:128])
                        copy(DVE, VA[1][:, ti, :], P1[:, 128:256])
                        act(siluz[:, t, :], P1[:, 256:512], AF.Silu)
                        act(betag[:, t, 0:2], P2[:, 0:2], AF.Sigmoid)
                        tt(DVE, sm[:, 0:2], P2[:, 2:4], dtb[:, :], ALU.add)
                        act(sm[:, 2:4], sm[:, 0:2], AF.Exp)
                        act(sm[:, 4:6], sm[:, 2:4], AF.Ln, bias=1.0)
                        tt(DVE, betag[:, t, 2:4], sm[:, 4:6], negea[:, :], ALU.mult)
                    if g == 0:
                        dump(qT[1][:, :], 1024, 512)
                        dump(KT[0][:, 0:512], 1536, 512)
                        dump(qbT[0][:, :], 2048, 512)
                        dump(kbT[1][:, :], 2560, 512)
                        dump(vbT[0][:, :], 3072, 512)
                        dump(VA[1][:, 1, :], 3584, 128)
                        dump(siluz[:, 2, :], 3712, 256)
                        dump(V(betag.t[:, :, :].rearrange("p a b -> p (a b)"), betag.k), 3968, 16)
                    K.barrier()
                    chk("1b")
                with ExitStack() as ec:
                    Ebuf = [K.sb([128, 512], F32, ec) for _ in range(2)]
                    SPb = [K.sb([128, 512], BF16, ec) for _ in range(2)]
                    tmpb = [K.sb([128, 512], F32, ec) for _ in range(2)]
                    Wb = [K.sb([128, 512], BF16, ec) for _ in range(2)]
                    Rbc = K.sb([128, 512], F32, ec)
                    for h in range(2):
                        memset(DVE, Rbc[:, :], 0.0)
                        PO = PS[6]
                        js = list(range(4 * g + 3, -1, -1))
                        nj = len(js)

                        def stA(idx, h=h, js=js):
                            j = js[idx]
                            b = idx % 2
                            r_ = j - 4 * g
                            kt = KT[h][:, j * 128:(j + 1) * 128]
                            mm(PS[b][:, :], kt, qT[h][:, :])
                            act(Ebuf[b][:, :], PS[b][:, :], AF.Exp)
                            act(SPb[b][:, :], Ebuf[b][:, :], AF.Ln, bias=1.0)
                            if r_ >= 0:
                                tt(DVE, SPb[b][:, :], SPb[b][:, :], mdiag[:, r_, :], ALU.mult)

                        def stB(idx, h=h, js=js):
                            j = js[idx]
                            b = idx % 2
                            r_ = j - 4 * g
                            PA, PB = PS[2 + b], PS[4 + b]
                            kt = KT[h][:, j * 128:(j + 1) * 128]
                            S_ = SPb[b]
                            mm(PA[:, :], kt, qT[h][:, :], start=True, stop=False, defer=True)
                            mm(PA[:, :], negU[:, :], S_[:, :], start=False, stop=True)
                            mm(PB[:, :], nones_b[:, :], S_[:, :])
                            T_ = tmpb[b]
                            tt(DVE, T_[:, :], PA[:, :], Rbc[:, :], ALU.add)
                            tt(DVE, Rbc[:, :], PB[:, :], Rbc[:, :], ALU.add)
                            W_ = Wb[b]
                            act(W_[:, :], T_[:, :], AF.Exp)
                            if r_ >= 0:
                                tt(DVE, W_[:, :], W_[:, :], mdiag[:, r_, :], ALU.mult)

                        def stC(idx, h=h, js=js, nj=nj):
                            j = js[idx]
                            b = idx % 2
                            mm(PO[:, :], VA[h][:, j, :], Wb[b][:, :], start=(idx == 0), stop=(idx == nj - 1))

                        if ATT_SKEW:
                            for s_ in range(nj + 2):
                                if s_ < nj:
                                    stA(s_)
                                if 0 <= s_ - 1 < nj:
                                    stB(s_ - 1)
                                if 0 <= s_ - 2 < nj:
                                    stC(s_ - 2)
                        else:
                            for s_ in range(nj):
                                stA(s_)
                                stB(s_)
                                stC(s_)
                        copy(ACT, stage[h][:, :], PO[:, :])
                        dma(SP, V(sendv[(g * 512) // TOWN, 0, h, :, (g * 512) % TOWN:(g * 512) % TOWN + 512], send.k),
                            stage[h][:, :], nowaw=True)
                    if g == 0:
                        dump(stage[0][:, :], 4096, 512)
                        dump(stage[1][:, :], 4608, 512)
                    K.barrier()
                    chk("1c")
                with ExitStack() as ed:
                    def f128(n, dt=F32):
                        return [K.sb([128, 128], dt, ed) for _ in range(n)]
                    G1, dl, du, Lm, Um, Ym, wv, tq, oraw, on_ = (f128(2) for _ in range(10))
                    PA_, PB_ = f128(4), f128(4)
                    attnT, Yb, wkT, kbt, kdec, vbt, ub, obb = (f128(2, BF16) for _ in range(8))
                    sc = K.sb([128, 32], F32, ed)
                    scn = [K.sb([128, 8], F32, ed) for _ in range(2)]
                    Pg = pslot(0, 0, 16)
                    SL = []
                    for h in range(2):
                        bank = 1 + h * 3
                        SL.append(dict(
                            Pdiff=pslot(bank, 0, 128), Pkk=pslot(bank, 128, 256), Pqk=pslot(bank, 256, 384),
                            PU=pslot(bank, 384, 512),
                            NP=[pslot(bank + 1, i * 128, (i + 1) * 128) for i in range(4)],
                            CP=[pslot(bank + 2, i * 128, (i + 1) * 128) for i in range(4)]))
                    for t in range(4):
                        ti = g * 4 + t
                        cols = slice(t * 128, (t + 1) * 128)
                        gsel = sc[:, 0:4]
                        for c_ in range(2):
                            ts(DVE, sc[:, 2 * c_:2 * c_ + 2], betag[:, t, 2:4], sel[:, c_:c_ + 1], None, ALU.mult)
                        mm(Pg.g((slice(None), slice(0, 2))), tri2[:, :], betag[:, t, 2:4])
                        mm(Pg.g((slice(None), slice(2, 4))), bones[:, :], betag[:, t, 2:4])
                        mm(Pg.g((slice(None), slice(4, 8))), ones_f[:, :], gsel)
                        gcs = sc[:, 4:12]
                        copy(ACT, gcs, Pg.g((slice(None), slice(0, 8))))
                        act(sc[:, 12:14], sc[:, 4:6], AF.Exp)
                        tt(DVE, sc[:, 14:16], sc[:, 6:8], sc[:, 4:6], ALU.subtract)
                        act(sc[:, 14:16], sc[:, 14:16], AF.Exp)
                        act(sc[:, 16:20], sc[:, 8:12], AF.Exp)
                        tt(DVE, sc[:, 20:22], betag[:, t, 0:2], sc[:, 12:14], ALU.mult)
                        if g == 0 and t == 0:
                            chk("d1")
                        def gdn_head(h, t=t, cols=cols, ti=ti):
                            beta_c = betag[:, t, h:h + 1]
                            g_c = betag[:, t, 2 + h:3 + h]
                            egc_c = sc[:, 12 + h:13 + h]
                            ekd_c = sc[:, 14 + h:15 + h]
                            bege_c = sc[:, 20 + h:21 + h]
                            Pdiff, Pkk, Pqk, PU = SL[h]["Pdiff"], SL[h]["Pkk"], SL[h]["Pqk"], SL[h]["PU"]
                            NP, CP = SL[h]["NP"], SL[h]["CP"]
                            A_ = (slice(None), slice(0, 128))
                            kT_t = kbT[h][:, cols]
                            qT_t = qbT[h][:, cols]
                            ts(DVE, G1[h][:, :], tri2[:, :], g_c, None, ALU.mult)
                            mm(Pdiff.g(A_), G1[h][:, :], ones_f[:, :], start=True, stop=False, defer=True)
                            mm(Pdiff.g(A_), nones_f[:, :], G1[h][:, :], start=False, stop=True)
                            mm(Pkk.g(A_), kT_t, kT_t)
                            mm(Pqk.g(A_), kT_t, qT_t)
                            yield
                            ts(DVE, dl[h][:, :], Pdiff.g(A_), 0.0, None, ALU.min)
                            ts(DVE, du[h][:, :], Pdiff.g(A_), -1.0, 0.0, ALU.mult, ALU.min)
                            act(dl[h][:, :], dl[h][:, :], AF.Exp)
                            act(du[h][:, :], du[h][:, :], AF.Exp)
                            yield
                            tt(DVE, dl[h][:, :], dl[h][:, :], mLs[:, :], ALU.mult)
                            tt(DVE, du[h][:, :], du[h][:, :], mUi[:, :], ALU.mult)
                            stt(Lm[h][:, :], Pkk.g(A_), beta_c, dl[h][:, :], ALU.mult, ALU.mult)
                            tt(DVE, attnT[h][:, :], Pqk.g(A_), du[h][:, :], ALU.mult)
                            yield
                            if g == 0 and t == 0 and h == 0:
                                chk("d2")
                            tr(PU.g(A_), Lm[h][:, :], ident_f[:, :])
                            copy(ACT, Um[h][:, :], PU.g(A_))
                            tt(DVE, Ym[h][:, :], ident_f[:, :], Um[h][:, :], ALU.subtract)
                            yield
                            if g == 0 and t == 0 and h == 0:
                                chk("d2b")
                            Pc, Ptc = Um[h], Lm[h]
                            for it in range(5):
                                Pn, Ptn = PA_[2 * h + it % 2], PB_[2 * h + it % 2]
                                s0, s1_, s2_ = NP[0], NP[1], NP[2]
                                if it < 4:
                                    mm(s0.g(A_), Ptc[:, :], Pc[:, :])
                                mm(s1_.g(A_), Pc[:, :], Ptc[:, :])
                                copy(ACT, Ptn[:, :], s1_.g(A_))
                                if it < 4:
                                    copy(ACT, Pn[:, :], s0.g(A_))
                                yield
                                mm(s2_.g(A_), Ptn[:, :], Ym[h][:, :])
                                tt(DVE, Ym[h][:, :], Ym[h][:, :], s2_.g(A_), ALU.add)
                                yield
                                Pc, Ptc = Pn, Ptn
                                if g == 0 and t == 0 and h == 0 and it == 0:
                                    chk("d2c")
                            copy(ACT, Yb[h][:, :], Ym[h][:, :])
                            yield
                            if g == 0 and t == 0 and h == 0:
                                chk("d3")
                            pb0, pb1 = PSB[0], PSB[1]
                            tr(V(PSBt[:, 0:128], pb0.k), kT_t, ident_b[:, :])
                            ts(DVE, kbt[h][:, :], V(PSBt[:, 0:128], pb0.k), bege_c, None, ALU.mult)
                            ts(DVE, kdec[h][:, :], V(PSBt[:, 0:128], pb0.k), ekd_c, None, ALU.mult)
                            yield
                            tr(V(PSBt[:, 512:640], pb1.k), vbT[h][:, cols], ident_b[:, :])
                            ts(DVE, vbt[h][:, :], V(PSBt[:, 512:640], pb1.k), beta_c, None, ALU.mult)
                            yield
                            mm(CP[0].g(A_), Yb[h][:, :], vbt[h][:, :])
                            copy(ACT, wv[h][:, :], CP[0].g(A_))
                            mm(CP[1].g(A_), kbt[h][:, :], Yb[h][:, :])
                            copy(ACT, wkT[h][:, :], CP[1].g(A_))
                            yield
                            if g == 0 and t == 0 and h == 0:
                                chk("d4")
                            for c_ in range(2):
                                pr = slice(c_ * 64, c_ * 64 + 64)
                                Ap = (pr, slice(0, 128))
                                Pu_, Pqs, Pau, PSs = CP[2], CP[3], NP[3], CP[0]
                                mm(Pu_.g(Ap), wkT[h][:, pr], Sb[h][:, :])
                                mm(Pqs.g(Ap), qbT[h][:, t * 128 + c_ * 64:t * 128 + c_ * 64 + 64], Sb[h][:, :])
                                tt(DVE, ub[h][pr, :], wv[h][pr, :], Pu_.g(Ap), ALU.subtract)
                                yield
                                mm(PSs.g(A_), kdec[h][pr, :], ub[h][pr, :])
                                mm(Pau.g(Ap), attnT[h][pr, pr], ub[h][pr, :])
                                act(tq[h][pr, :], Pqs.g(Ap), AF.Identity, scale=V(egc_c.ap[pr, :], egc_c.k))
                                tt(DVE, oraw[h][pr, :], tq[h][pr, :], Pau.g(Ap), ALU.add)
                                yield
                                egl_c = sc[:, 16 + 2 * c_ + h:17 + 2 * c_ + h]
                                stt(Sb[h][:, :], Sf[h][:, :], egl_c, PSs.g(A_), ALU.mult, ALU.add)
                                stt(Sf[h][:, :], Sf[h][:, :], egl_c, PSs.g(A_), ALU.mult, ALU.add)
                                yield
                                if g == 0 and t == 0 and h == 0 and c_ == 0:
                                    chk("d5a")
                                if g == 0 and t == 0 and h == 0 and c_ == 1:
                                    chk("d5")
                            act(on_[h][:, :], oraw[h][:, :], AF.Square, accum=scn[h][:, 0:1])
                            ts(DVE, scn[h][:, 1:2], scn[h][:, 0:1], 1.0 / 128, EPS, ALU.mult, ALU.add)
                            act(scn[h][:, 2:3], scn[h][:, 1:2], AF.Ln)
                            act(scn[h][:, 3:4], scn[h][:, 2:3], AF.Exp, scale=-0.5)
                            stt(on_[h][:, :], oraw[h][:, :], scn[h][:, 3:4], onw[:, :], ALU.mult, ALU.mult)
                            yield
                            tt(DVE, obb[h][:, :], on_[h][:, :], siluz[:, t, h * 128:(h + 1) * 128], ALU.mult)
                            tr(V(PSBt[:, 128:256], pb0.k), obb[h][:, :], ident_b[:, :])
                            copy(DVE, stage[2 + h][:, cols], V(PSBt[:, 128:256], pb0.k))
                            if g == 0 and t == 0 and h == 0:
                                chk("d6")
                        gens = [gdn_head(0), gdn_head(1)]
                        if GDN_MODE == 0:
                            for gen_ in gens:
                                for _ in gen_:
                                    pass
                            gens = []
                        while gens:
                            for gen_ in list(gens):
                                try:
                                    next(gen_)
                                except StopIteration:
                                    gens.remove(gen_)
                    for h in range(2):
                        dma(SP, V(sendv[(g * 512) // TOWN, 1, h, :, (g * 512) % TOWN:(g * 512) % TOWN + 512], send.k),
                            stage[2 + h][:, :], nowaw=True)
                    if g == 0:
                        dump(stage[2][:, :], 5120, 512)
                        dump(stage[3][:, :], 5632, 512)
                    K.barrier()
                    chk("1d")
            K.barrier()

        POOL.issue(lambda: nc.gpsimd.collective_compute(
            "AllGather", ALU.bypass, replica_groups=[list(range(NCORE))], ins=[send.t], outs=[recv.t]),
            [send.k], [recv.k])
        POOL._wait(POOL.sid, POOL.cnt)
        chk("ag")
        rank = nc.gpsimd.partition_id()
        rv = recv.t.rearrange("(s t a h p) c -> t a p s h c", s=8, t=8, a=2, h=2)

        with ExitStack() as e2:
            g1bc = K.sb([128, D], F32, e2)
            g2bc = K.sb([128, D], F32, e2)
            with ExitStack() as eg:
                dg = [K.sb([128, 128], F32, eg) for _ in range(2)]
                for which, dst in ((32, g1bc), (80, g2bc)):
                    for j in range(NK):
                        d_ = dg[j % 2]
                        ts(DVE, d_[:, :], ident_f[:, :], modc[:, which + j:which + j + 1], None, ALU.mult)
                        P = PS[j % 2]
                        mm(P[:, 0:128], ones_f[:, :], d_[:, :])
                        copy(ACT, dst[:, j * 128:(j + 1) * 128], P[:, 0:128])
                K.barrier()
            for half in range(NHALF):
                hc = slice(half * 512, half * 512 + 512)
                with ExitStack() as eh:
                    h1 = [K.sb([128, D], F32, eh) for _ in range(4)]
                    mT = K.sb([128, NK, 512], BF16, eh)
                    with ExitStack() as eb:
                        oaT = K.sb([128, NK, 512], BF16, eb)
                        obT = K.sb([128, NK, 512], BF16, eb)
                        uTo = K.sb([128, NK, 512], BF16, eb)
                        for ab, dst in ((0, oaT), (1, obT)):
                            src = rv[bass.ds(rank, 1), ab].rearrange("o p s h c -> (o p) s h c")
                            for hh in range(2):
                                POOL.issue(lambda _d=dst, _s=src, _h=hh: nc.gpsimd.dma_start(
                                    out=_d.t[:, :, :].rearrange("p (s h) c -> p s h c", s=8)[:, :, _h, :],
                                    in_=_s[:, :, _h, hc]),
                                    [recv.k], [dst.k], dma_k=dst.k, nowaw=True)
                        with ExitStack() as ea:
                            def load_xo(t, dst, _half=half):
                                r0 = _half * 512 + t * 128
                                dma(SP, dst[:, :], x_own[r0:r0 + 128, :])
                                return dst
                            make_uT(ea, load_xo, 4, a1, b1, uTo)
                            K.barrier()
                        for t in range(4):
                            r0 = half * 512 + t * 128
                            dma(SP, h1[t][:, :], x_own[r0:r0 + 128, :])
                        with ExitStack() as ew:
                            slab = [[K.sb([128, NK, 256], BF16, ew) for _ in range(4)] for _ in range(2)]
                            Fm = [K.sb([128, 512], F32, ew) for _ in range(4)]
                            srcs = [(p_a, 0), (p_b, 0), (w_g, 0), (w_g, D)]
                            for c2 in range(8):
                                sl_ = slab[c2 % 2]
                                for wi, (wt, off) in enumerate(srcs):
                                    wvv = wt.t.rearrange("(k p) n -> p k n", p=128)
                                    dma(POOL, sl_[wi][:, :, :], V(wvv[:, :, off + c2 * 256:off + (c2 + 1) * 256], wt.k))
                                for cc_ in range(2):
                                    jc = c2 * 2 + cc_
                                    wc = slice(cc_ * 128, cc_ * 128 + 128)
                                    Ps = [PS[(jc % 2) * 3 + i] if i < 3 else PS[6] for i in range(4)]
                                    acts = [oaT, obT, uTo, uTo]
                                    for wi in range(4):
                                        for k in range(NK):
                                            mm(Ps[wi][:, :], sl_[wi][:, k, wc], acts[wi][:, k, :],
                                               start=(k == 0), stop=(k == NK - 1), defer=(k != NK - 1))
                                    act(Fm[0][:, :], Ps[2][:, :], AF.Sigmoid)
                                    act(Fm[1][:, :], Ps[3][:, :], AF.Sigmoid)
                                    tt(DVE, Fm[2][:, :], Fm[0][:, :], Ps[0][:, :], ALU.mult)
                                    tt(DVE, Fm[3][:, :], Fm[1][:, :], Ps[1][:, :], ALU.mult)
                                    tt(DVE, mT[:, jc, :], Fm[2][:, :], Fm[3][:, :], ALU.add)
                            K.barrier()
                    with ExitStack() as ew:
                        slab = [K.sb([128, NK, 512], BF16, ew) for _ in range(2)]
                        Fm = [K.sb([128, 512], F32, ew) for _ in range(2)]
                        wvv = w_out.t.rearrange("(k p) n -> p k n", p=128)
                        for cg in range(4):
                            sl_ = slab[cg % 2]
                            cs_ = slice(cg * 512, cg * 512 + 512)
                            dma(POOL, sl_[:, :, :], V(wvv[:, :, cs_], w_out.k))
                            for t in range(4):
                                P = PS[(cg * 4 + t) % 4]
                                for k in range(NK):
                                    mm(P[:, :], mT[:, k, t * 128:(t + 1) * 128], sl_[:, k, :],
                                       start=(k == 0), stop=(k == NK - 1), defer=(k != NK - 1))
                                f_ = Fm[t % 2]
                                tt(DVE, f_[:, :], P[:, :], g1bc[:, cs_], ALU.mult)
                                tt(DVE, h1[t][:, cs_], f_[:, :], h1[t][:, cs_], ALU.add)
                        K.barrier()
                    with ExitStack() as ef:
                        u2T = K.sb([128, NK, 512], BF16, ef)
                        ffT = K.sb([128, NF, 512], BF16, ef)
                        with ExitStack() as ea:
                            def load_h(t, dst):
                                return h1[t]
                            make_uT(ea, load_h, 4, a2, b2, u2T)
                            K.barrier()
                        with ExitStack() as ew:
                            slab = [[K.sb([128, NK, 256], BF16, ew) for _ in range(2)] for _ in range(2)]
                            Fm = [K.sb([128, 512], F32, ew) for _ in range(2)]
                            wgv = w_gate.t.rearrange("(k p) n -> p k n", p=128)
                            wuv = w_up.t.rearrange("(k p) n -> p k n", p=128)
                            for f2 in range(NF // 2):
                                sl_ = slab[f2 % 2]
                                cs_ = slice(f2 * 256, f2 * 256 + 256)
                                dma(POOL, sl_[0][:, :, :], V(wgv[:, :, cs_], w_gate.k))
                                dma(POOL, sl_[1][:, :, :], V(wuv[:, :, cs_], w_up.k))
                                for cc_ in range(2):
                                    f = f2 * 2 + cc_
                                    wc = slice(cc_ * 128, cc_ * 128 + 128)
                                    Pg_, Pu2 = PS[(f % 2) * 2], PS[(f % 2) * 2 + 1]
                                    for wi, P in ((0, Pg_), (1, Pu2)):
                                        for k in range(NK):
                                            mm(P[:, :], sl_[wi][:, k, wc], u2T[:, k, :],
                                               start=(k == 0), stop=(k == NK - 1), defer=(k != NK - 1))
                                    f_ = Fm[f % 2]
                                    act(f_[:, :], Pg_[:, :], AF.Silu)
                                    tt(DVE, ffT[:, f, :], f_[:, :], Pu2[:, :], ALU.mult)
                            K.barrier()
                        with ExitStack() as ew:
                            slab = [K.sb([128, NF, 256], BF16, ew) for _ in range(2)]
                            Fm = [K.sb([128, 256], F32, ew) for _ in range(2)]
                            wdv = w_down.t.rearrange("(f p) n -> p f n", p=128)
                            for cg in range(8):
                                sl_ = slab[cg % 2]
                                cs_ = slice(cg * 256, cg * 256 + 256)
                                dma(POOL, sl_[:, :, :], V(wdv[:, :, cs_], w_down.k))
                                for t in range(4):
                                    P = PS[(cg * 4 + t) % 4]
                                    for f in range(NF):
                                        mm(P[:, 0:256], ffT[:, f, t * 128:(t + 1) * 128], sl_[:, f, :],
                                           start=(f == 0), stop=(f == NF - 1), defer=(f != NF - 1))
                                    f_ = Fm[t % 2]
                                    tt(DVE, f_[:, :], P[:, 0:256], g2bc[:, cs_], ALU.mult)
                                    tt(DVE, h1[t][:, cs_], f_[:, :], h1[t][:, cs_], ALU.add)
                            K.barrier()
                    for t in range(4):
                        r0 = half * 512 + t * 128
                        dma(SP, out_d[r0:r0 + 128, :], h1[t][:, :], nowaw=True)
                    K.barrier()
                    for sid, val in out_d.k.w.items():
                        SP._wait(sid, val)
    return nc


def _consts():
    i = np.arange(128)
    same = (i[:, None] // 64) == (i[None, :] // 64)
    ident = np.eye(128, dtype=np.float32)
    tri2 = ((i[:, None] <= i[None, :]) & same).astype(np.float32)
    bones = same.astype(np.float32)
    mLs = ((i[:, None] > i[None, :]) & same).astype(np.float32)
    mUi = ((i[None, :] >= i[:, None]) & same).astype(np.float32)
    negU = -(i[:, None] >= i[None, :]).astype(np.float32)
    tq = np.arange(512)
    md = [((r * 128 + i[:, None]) < tq[None, :]).astype(np.float32) for r in range(4)]
    sel = np.stack([(i // 64) == 0, (i // 64) == 1], axis=1).astype(np.float32)
    return np.ascontiguousarray(np.concatenate([ident, tri2, bones, mLs, mUi, negU] + md + [sel], axis=1))


_NC_CACHE = {}


def kernel(x, c, w_mod, b_mod, norm1_w, w_in, q_norm_w, k_norm_w, conv_w, a_log, dt_bias,
           o_norm_w, p_a, p_b, w_out, norm2_w, w_gate, w_up, w_down):
    f = lambda a: np.ascontiguousarray(np.asarray(a, dtype=np.float32))
    x = f(x)
    SEQ = x.shape[1]
    TOWN = SEQ // NCORE
    xa = x[0]
    col16 = lambda v: f(np.asarray(v).reshape(NK, 128).T)
    w_in0 = f(w_in)[0]
    conv0 = f(conv_w)[0]
    wm0 = f(w_mod)[0]
    bm0 = f(np.asarray(b_mod)[0].reshape(96, 128).T)
    shared = {
        "x_all": xa,
        "c_col": col16(np.asarray(c)[0]),

        "n1w_col": col16(np.asarray(norm1_w)[0]),
        "n2w_col": col16(np.asarray(norm2_w)[0]),
        "w_g": f(w_in0[:, 14368:18464]),
        "qnw_col": f(np.asarray(q_norm_w)[0].reshape(128, 1)),
        "knw_col": f(np.asarray(k_norm_w)[0].reshape(128, 1)),
        "onw_bc": f(np.broadcast_to(np.asarray(o_norm_w)[0][None, :], (128, 128))),
        "p_a": f(p_a)[0], "p_b": f(p_b)[0], "w_out": f(w_out)[0],
        "w_gate": f(w_gate)[0], "w_up": f(w_up)[0], "w_down": f(w_down)[0],
        "cst": _consts(),
    }
    if STOP is not None:
        for k_ in ("w_g", "p_a", "p_b", "w_out", "w_gate", "w_up", "w_down"):
            shared.pop(k_)
    in_maps = []
    for core in range(NCORE):
        hs = [2 * core, 2 * core + 1]
        cols = []
        for base in (0, 2048):
            for h in hs:
                cols.append(np.arange(base + h * 128, base + (h + 1) * 128))
        for base in (6144, 6144 + 2048, 6144 + 4096):
            for h in hs:
                cols.append(np.arange(base + h * 128, base + (h + 1) * 128))
        for base in (4096, 12288):
            for h in hs:
                cols.append(np.arange(base + h * 128, base + (h + 1) * 128))
        cols.append(np.array([14336 + hs[0], 14336 + hs[1], 14352 + hs[0], 14352 + hs[1]]))
        cols = np.concatenate(cols)
        cw = np.zeros((128, 24), np.float32)
        for hi, h in enumerate(hs):
            for typ in range(3):
                blk = conv0[:, typ * 2048 + h * 128: typ * 2048 + (h + 1) * 128]
                cw[:, (hi * 3 + typ) * 4:(hi * 3 + typ) * 4 + 4] = blk.T
        m = dict(shared)
        m["x_own"] = np.ascontiguousarray(xa[core * TOWN:(core + 1) * TOWN])
        m["w_mod"] = np.ascontiguousarray(wm0[:, core * 1536:(core + 1) * 1536])
        m["bmod_col"] = np.ascontiguousarray(bm0[:, core * 12:(core + 1) * 12])
        m["w_h"] = np.ascontiguousarray(w_in0[:, cols])
        m["convw"] = cw
        m["alog_bc"] = f(np.broadcast_to(np.asarray(a_log)[0][hs][None, :], (128, 2)))
        m["dtb_bc"] = f(np.broadcast_to(np.asarray(dt_bias)[0][hs][None, :], (128, 2)))
        in_maps.append(m)
    if SEQ not in _NC_CACHE:
        _NC_CACHE[SEQ] = build(SEQ)
    nc = _NC_CACHE[SEQ]
    res = run_bass_kernel_spmd(nc, in_maps, core_ids=list(range(NCORE)))
    out = np.concatenate([np.asarray(r["out"]) for r in res.results], axis=0)
    return out.reshape(1, SEQ, D).astype(np.float32)
```
